# Optimizing a Trainium2 kernel written in Bass

```python
import jax, jax.numpy as jnp
from jax import lax
import numpy as np

D_MODEL = 1024
BATCH = 8
SEQ = 4096
DEPTH = 2

GRID_W = 64
N_META = 16
HEAD_DIM = 64
ATTN_HEADS = 8
ATTN_KV_HEADS = 2
ATTN_GROUP = ATTN_HEADS // ATTN_KV_HEADS
ATTN_WIDTH = ATTN_HEADS * HEAD_DIM
ATTN_KV_WIDTH = ATTN_KV_HEADS * HEAD_DIM
NA_HEADS = 8
NA_WIDTH = NA_HEADS * HEAD_DIM
NA_KH_MAX = 8
NA_KW = 16
Q_BLOCK = 128
ROPE_THETA = 10000.0
EPS = 1e-6
IN_SPLITS = (ATTN_WIDTH, ATTN_KV_WIDTH, ATTN_KV_WIDTH, ATTN_WIDTH,
             NA_WIDTH, NA_WIDTH, NA_WIDTH, NA_WIDTH, D_MODEL, D_MODEL)
IN_COLS = ATTN_WIDTH * 2 + ATTN_KV_WIDTH * 2 + NA_WIDTH * 4 + D_MODEL * 2

kernel_name = "hybrid_gqa_natten_gated_encoder"


def rms_norm(x, g):
    xf = x.astype(jnp.float32)
    y = xf * lax.rsqrt(jnp.mean(xf * xf, axis=-1, keepdims=True) + EPS)
    return (y * g.astype(jnp.float32)).astype(x.dtype)


def split_cols(p):
    outs, off = [], 0
    for w in IN_SPLITS:
        outs.append(p[..., off:off + w])
        off += w
    return outs


def axial_rope_tables(n_real):
    t = jnp.arange(n_real, dtype=jnp.int32)
    zeros = jnp.zeros((N_META,), jnp.int32)
    row = jnp.concatenate([zeros, t // GRID_W]).astype(jnp.float32)
    col = jnp.concatenate([zeros, t % GRID_W]).astype(jnp.float32)
    axis_dim = HEAD_DIM // 2
    inv = ROPE_THETA ** (-jnp.arange(0, axis_dim, 2, dtype=jnp.float32) / axis_dim)
    ang_r = row[:, None] * inv[None]
    ang_c = col[:, None] * inv[None]
    ang = jnp.concatenate([ang_r, ang_r, ang_c, ang_c], axis=-1)
    return jnp.cos(ang), jnp.sin(ang)


def apply_axial_rope(x, cos, sin):
    axis_dim = HEAD_DIM // 2
    q4 = axis_dim // 2
    xf = x.astype(jnp.float32)
    xr, xc = xf[..., :axis_dim], xf[..., axis_dim:]
    rot = jnp.concatenate([-xr[..., q4:], xr[..., :q4], -xc[..., q4:], xc[..., :q4]], axis=-1)
    return (xf * cos[None, :, None, :] + rot * sin[None, :, None, :]).astype(x.dtype)


def gqa_attention(q, k, v):
    b, l = q.shape[:2]
    n = l - N_META
    scale = HEAD_DIM ** -0.5

    def attend(qb):
        s = jnp.einsum('bqkgd,bskd->bkgqs', qb, k).astype(jnp.float32) * scale
        p = jax.nn.softmax(s, axis=-1).astype(v.dtype)
        return jnp.einsum('bkgqs,bskd->bqkgd', p, v)

    qg = q.reshape(b, l, ATTN_KV_HEADS, ATTN_GROUP, HEAD_DIM)
    o_meta = attend(qg[:, :N_META])
    q_blocks = qg[:, N_META:].reshape(b, n // Q_BLOCK, Q_BLOCK, ATTN_KV_HEADS, ATTN_GROUP, HEAD_DIM)
    q_blocks = q_blocks.transpose(1, 0, 2, 3, 4, 5)
    o_real = lax.map(attend, q_blocks).transpose(1, 0, 2, 3, 4, 5)
    o_real = o_real.reshape(b, n, ATTN_KV_HEADS, ATTN_GROUP, HEAD_DIM)
    return jnp.concatenate([o_meta, o_real], axis=1).reshape(b, l, ATTN_WIDTH)


def neighbourhood_attention(q, k, v, rpb):
    b, l = q.shape[:2]
    n = l - N_META
    rows = n // GRID_W
    kh = min(NA_KH_MAX, rows)
    scale = HEAD_DIM ** -0.5
    qm, km, vm = q[:, :N_META], k[:, :N_META], v[:, :N_META]

    s_mm = jnp.einsum('bqhd,bmhd->bhqm', qm, km).astype(jnp.float32) * scale
    o_meta = jnp.einsum('bhqm,bmhd->bqhd', jax.nn.softmax(s_mm, axis=-1).astype(v.dtype), vm)

    qg = q[:, N_META:].reshape(b, rows, GRID_W, NA_HEADS, HEAD_DIM)
    kg = k[:, N_META:].reshape(b, rows, GRID_W, NA_HEADS, HEAD_DIM)
    vg = v[:, N_META:].reshape(b, rows, GRID_W, NA_HEADS, HEAD_DIM)

    r = jnp.arange(rows)
    row_start = jnp.clip(r - kh // 2, 0, rows - kh)
    row_bias_idx = row_start[:, None] + jnp.arange(kh)[None] - r[:, None] + NA_KH_MAX - 1
    c = jnp.arange(GRID_W)
    col_start = jnp.clip(c - NA_KW // 2, 0, GRID_W - NA_KW)
    col_idx = col_start[:, None] + jnp.arange(NA_KW)[None]
    col_bias_idx = col_idx - c[:, None] + NA_KW - 1
    rpb_cols = rpb[:, :, col_bias_idx]

    def row_step(args):
        q_row, rs, rbi = args
        k_band = lax.dynamic_slice_in_dim(kg, rs, kh, axis=1)
        v_band = lax.dynamic_slice_in_dim(vg, rs, kh, axis=1)
        k_win = k_band[:, :, col_idx]
        v_win = v_band[:, :, col_idx]
        s_win = jnp.einsum('bchd,bicjhd->bhcij', q_row, k_win).astype(jnp.float32) * scale
        bias = rpb_cols[:, rbi].transpose(0, 2, 1, 3)
        s_win = s_win + bias[None].astype(jnp.float32)
        s_meta = jnp.einsum('bchd,bmhd->bhcm', q_row, km).astype(jnp.float32) * scale
        s = jnp.concatenate([s_win.reshape(b, NA_HEADS, GRID_W, kh * NA_KW), s_meta], axis=-1)
        p = jax.nn.softmax(s, axis=-1).astype(v.dtype)
        p_win = p[..., :kh * NA_KW].reshape(b, NA_HEADS, GRID_W, kh, NA_KW)
        p_meta = p[..., kh * NA_KW:]
        return (jnp.einsum('bhcij,bicjhd->bchd', p_win, v_win)
                + jnp.einsum('bhcm,bmhd->bchd', p_meta, vm))

    o_rows = lax.map(row_step, (qg.transpose(1, 0, 2, 3, 4), row_start, row_bias_idx))
    o_real = o_rows.transpose(1, 0, 2, 3, 4).reshape(b, n, NA_WIDTH)
    return jnp.concatenate([o_meta.reshape(b, N_META, NA_WIDTH), o_real], axis=1)


def hybrid_layer(x, norm_g, w_in, q_norm_g, k_norm_g, rpb, w_o_attn, w_o_na, w_out, cos, sin):
    b, l, _ = x.shape
    h = rms_norm(x, norm_g)
    proj = jnp.einsum('bld,de->ble', h, w_in)
    qa, ka, va, za, qb, kb, vb, zb, ga, gb = split_cols(proj)

    qa = apply_axial_rope(rms_norm(qa.reshape(b, l, ATTN_HEADS, HEAD_DIM), q_norm_g), cos, sin)
    ka = apply_axial_rope(rms_norm(ka.reshape(b, l, ATTN_KV_HEADS, HEAD_DIM), k_norm_g), cos, sin)
    va = va.reshape(b, l, ATTN_KV_HEADS, HEAD_DIM)
    oa = gqa_attention(qa, ka, va) * jax.nn.silu(za)

    qb = qb.reshape(b, l, NA_HEADS, HEAD_DIM)
    kb = kb.reshape(b, l, NA_HEADS, HEAD_DIM)
    vb = vb.reshape(b, l, NA_HEADS, HEAD_DIM)
    ob = neighbourhood_attention(qb, kb, vb, rpb) * jax.nn.silu(zb)

    ya = jnp.einsum('ble,ed->bld', oa, w_o_attn)
    yb = jnp.einsum('ble,ed->bld', ob, w_o_na)
    mixed = jax.nn.sigmoid(ga) * ya + jax.nn.sigmoid(gb) * yb
    return x + jnp.einsum('bld,de->ble', mixed, w_out)


def setup_inputs(seed: int = 0) -> dict:
    key = jax.random.key(seed)
    ks = jax.random.split(key, 12)
    f32 = jnp.float32
    x = jax.random.normal(ks[0], (BATCH, SEQ, D_MODEL), f32)
    meta_tokens = jax.random.normal(ks[1], (N_META, D_MODEL), f32)
    norm_g = 1.0 + 0.05 * jax.random.normal(ks[2], (DEPTH, D_MODEL), f32)
    w_in = jax.random.normal(ks[3], (DEPTH, D_MODEL, IN_COLS), f32) * D_MODEL ** -0.5
    q_norm_g = 1.0 + 0.05 * jax.random.normal(ks[4], (DEPTH, HEAD_DIM), f32)
    k_norm_g = 1.0 + 0.05 * jax.random.normal(ks[5], (DEPTH, HEAD_DIM), f32)
    na_rpb = 0.1 * jax.random.normal(ks[6], (DEPTH, NA_HEADS, 2 * NA_KH_MAX - 1, 2 * NA_KW - 1), f32)
    w_o_attn = jax.random.normal(ks[7], (DEPTH, ATTN_WIDTH, D_MODEL), f32) * ATTN_WIDTH ** -0.5
    w_o_na = jax.random.normal(ks[8], (DEPTH, NA_WIDTH, D_MODEL), f32) * NA_WIDTH ** -0.5
    w_out = jax.random.normal(ks[9], (DEPTH, D_MODEL, D_MODEL), f32) * D_MODEL ** -0.5
    final_norm_g = 1.0 + 0.05 * jax.random.normal(ks[10], (D_MODEL,), f32)
    return {"x": x, "meta_tokens": meta_tokens, "norm_g": norm_g, "w_in": w_in,
            "q_norm_g": q_norm_g, "k_norm_g": k_norm_g, "na_rpb": na_rpb,
            "w_o_attn": w_o_attn, "w_o_na": w_o_na, "w_out": w_out,
            "final_norm_g": final_norm_g}


def reference(x, meta_tokens, norm_g, w_in, q_norm_g, k_norm_g, na_rpb, w_o_attn, w_o_na, w_out, final_norm_g):
    b, s, _ = x.shape
    meta = jnp.broadcast_to(meta_tokens[None].astype(x.dtype), (b, N_META, D_MODEL))
    h = jnp.concatenate([meta, x], axis=1)
    cos, sin = axial_rope_tables(s)
    for i in range(DEPTH):
        h = hybrid_layer(h, norm_g[i], w_in[i], q_norm_g[i], k_norm_g[i], na_rpb[i],
                         w_o_attn[i], w_o_na[i], w_out[i], cos, sin)
    h = rms_norm(h, final_norm_g)
    return h[:, N_META:]
```

```python
import contextlib
import numpy as np
import concourse.bass as bass
import concourse.mybir as mybir
from concourse.bass_utils import run_bass_kernel_spmd

F32 = mybir.dt.float32
BF16 = mybir.dt.bfloat16
AF = mybir.ActivationFunctionType
ALU = mybir.AluOpType

NCORES = 8
DM = 1024
SEQ = 4096
NMETA = 16
LT = SEQ + NMETA
DEPTH = 2
INC = 5376
EPS = 1e-6
NEG = -30000.0
C_QA, C_KA, C_VA, C_ZA, C_QB, C_KB, C_VB, C_ZB, C_GA, C_GB = 0, 512, 640, 768, 1280, 1792, 2304, 2816, 3328, 4352
ENGS = ("pe", "act", "dve", "pool", "sp")
NRING = 6


class _Op:
    __slots__ = ("eng", "fn", "waits", "sig", "is_dma", "key", "idx", "ph")

    def __init__(self, eng, fn, is_dma, key):
        self.eng, self.fn, self.is_dma, self.key = eng, fn, is_dma, key
        self.waits = []
        self.sig = None


class Prog:
    def __init__(self):
        self.ops = []
        self.last_w = {}
        self.readers = {}
        self.deps = []
        self.phase = ""

    def _add(self, eng, fn, reads, writes, is_dma, key):
        o = _Op(eng, fn, is_dma, key)
        o.idx = len(self.ops)
        o.ph = self.phase
        self.ops.append(o)
        prods = set()
        for r in reads:
            w = self.last_w.get(r)
            if w is not None:
                prods.add(w)
        for r in writes:
            w = self.last_w.get(r)
            if w is not None:
                prods.add(w)
            prods.update(self.readers.get(r, ()))
        prods.discard(o)
        for p in prods:
            if (not p.is_dma) and (not is_dma) and p.eng == "pe" and eng == "pe":
                continue
            self.deps.append((p, o))
        for r in reads:
            self.readers.setdefault(r, []).append(o)
        for r in writes:
            self.last_w[r] = o
            self.readers[r] = []
        return o

    def op(self, eng, fn, reads=(), writes=()):
        return self._add(eng, fn, tuple(reads), tuple(writes), False, None)

    def dma(self, eng, fn, reads=(), writes=(), key=None):
        if key is None:
            key = writes[0]
        return self._add(eng, fn, tuple(reads), tuple(writes), True, ("dma", key))

    def emit(self, nc):
        prod_set = set(p for p, _ in self.deps)
        counters = {}
        hist = {}
        for o in self.ops:
            if o in prod_set or o.is_dma:
                sk = o.key if o.is_dma else ("eng", o.eng)
                counters[sk] = counters.get(sk, 0) + (16 if o.is_dma else 1)
                o.sig = (sk, counters[sk])
                if o.is_dma:
                    hist.setdefault(sk, []).append((o.idx, counters[sk]))
        by_cons = {}
        for p, c in self.deps:
            by_cons.setdefault(c, []).append(p)
        waited = {e: {} for e in ENGS}
        for o in self.ops:
            need = {}
            for p in by_cons.get(o, ()):
                sk, v = p.sig
                if p.is_dma:
                    for (ii, vv) in hist[sk]:
                        if ii < o.idx:
                            v = max(v, vv)
                        else:
                            break
                if v > need.get(sk, 0):
                    need[sk] = v
            for sk, v in need.items():
                if waited[o.eng].get(sk, 0) >= v:
                    continue
                waited[o.eng][sk] = v
                o.waits.append((sk, v))
        semkeys = list(counters.keys())
        with contextlib.ExitStack() as st:
            sems = {}
            for i, sk in enumerate(semkeys):
                sems[sk] = st.enter_context(nc.semaphore("s%d" % i))
            block = st.enter_context(nc.Block())
            per = {e: [o for o in self.ops if o.eng == e] for e in ENGS}

            def run(eng_obj, lst):
                for o in lst:
                    for sk, v in o.waits:
                        eng_obj.wait_ge(sems[sk], v)
                    ins = o.fn(eng_obj)
                    if o.sig is not None:
                        ins.then_inc(sems[o.sig[0]], 16 if o.is_dma else 1)

            @block.tensor
            def _(e):
                run(e, per["pe"])

            @block.scalar
            def _(e):
                run(e, per["act"])

            @block.vector
            def _(e):
                run(e, per["dve"])

            @block.gpsimd
            def _(e):
                run(e, per["pool"])

            @block.sync
            def _(e):
                run(e, per["sp"])
                for sk in semkeys:
                    if sk[0] == "dma":
                        e.wait_ge(sems[sk], counters[sk])


def _groups():
    gs = []
    for g in range(8):
        gs.append((512 * g, 512, [(512 * g + 128 * i, 128) for i in range(4)]))
    gs.append((SEQ, NMETA, [(SEQ, NMETA)]))
    return gs


def build_nc(depth=DEPTH, debug=False):
    nc = bass.Bass("TRN2", target_bir_lowering=False)

    def din(name, shape):
        return nc.dram_tensor(name, shape, F32, kind="ExternalInput").ap()

    x = din("x", [SEQ, DM])
    meta = din("meta", [NMETA, DM])
    ng = din("ng", [DEPTH, 128, DM])
    w_in = din("w_in", [DEPTH, DM, INC])
    qg = din("qg", [DEPTH, 128, 1])
    kg = din("kg", [DEPTH, 128, 1])
    mtab = din("mtab", [DEPTH, 16, 64, 2, 4, 64])
    w_oa = din("w_oa", [DEPTH, 512, DM])
    w_ob = din("w_ob", [DEPTH, 512, DM])
    w_out = din("w_out", [DEPTH, DM, DM])
    fg = din("fg", [128, DM])
    cosT = din("cosT", [128, LT])
    sinT = din("sinT", [128, LT])
    rmat = din("rmat", [128, 128])
    bones = din("bones", [128, 128])
    eye = din("eye", [128, 128])
    vpat = din("vpat", [128, 194])
    out = nc.dram_tensor("out", [SEQ, DM], F32, kind="ExternalOutput").ap()
    hbuf = nc.dram_tensor("hbuf", [LT, DM], F32).ap()
    hnts = nc.dram_tensor("hnts", [128, 8, LT], BF16).ap()
    kbs = nc.dram_tensor("kbs", [128, 4, LT], BF16).ap()
    vbs = nc.dram_tensor("vbs", [LT, 512], BF16).ap()
    dk = "ExternalOutput" if debug else "Internal"
    yas = nc.dram_tensor("yas", [128, 8, LT], F32, kind=dk).ap()
    ybs = nc.dram_tensor("ybs", [128, 8, LT], F32, kind=dk).ap()

    P = Prog()
    dbg_list = []

    def dbg(name, ap, reads, dt):
        if not debug:
            return
        shape = list(ap.shape)
        d = nc.dram_tensor("dbg_" + name, shape, dt, kind="ExternalOutput").ap()
        P.dma("sp", lambda e: e.dma_start(out=d, in_=ap), reads=reads, writes=[("dbg", name)], key=("dbg", len(dbg_list) % 4))
        dbg_list.append(name)

    with contextlib.ExitStack() as st:
        def sb(name, shape, dt):
            return st.enter_context(nc.sbuf_tensor(name, shape, dt))

        def ps(name, shape, dt):
            return st.enter_context(nc.psum_tensor(name, shape, dt))

        WIN = sb("WIN", [128, 8, 2048], BF16)
        W2 = sb("W2", [128, 8, 1024], BF16)
        KAT = sb("KAT", [128, LT], BF16)
        VA1 = sb("VA1", [128, 33, 194], BF16)
        BI = sb("BI", [128, 5, 2, 4, 128], BF16)
        BB = sb("BB", [128, 2, 2, 4, 128], BF16)
        KBR = sb("KBR", [128, 4, NRING * 128], BF16)
        VBR = sb("VBR", [128, NRING, 4, 194], BF16)
        KBM = sb("KBM", [128, 4, 16], BF16)
        VBM = sb("VBM", [16, 4, 194], BF16)
        HN = [sb("HN%d" % i, [128, DM], BF16) for i in range(2)]
        HNT = sb("HNT", [128, 8, 512], BF16)
        QAT = sb("QAT", [128, 4, 512], BF16)
        QBT = sb("QBT", [128, 4, 512], BF16)
        PTS = [sb("PTS%d" % i, [128, 2, 512], BF16) for i in range(3)]
        NF2 = 14
        COSB = sb("COSB", [128, 512], F32)
        SINB = sb("SINB", [128, 512], F32)
        F2 = [sb("F2_%d" % i, [128, 512], F32) for i in range(NF2)]
        NF4 = 4
        F4 = [sb("F4_%d" % i, [128, 2, 512], F32) for i in range(NF4)]
        OA = sb("OA", [128, 8, 512], BF16)
        OB = sb("OB", [128, 4, 512], BF16)
        GV = sb("GV", [128, DM], F32)
        IDB = sb("IDB", [128, 128], BF16)
        BON = sb("BON", [128, 128], F32)
        RMT = sb("RMT", [128, 128], F32)
        SEL = sb("SEL", [128, 64], F32)
        DMY = sb("DMY", [1, 8], F32)
        PATT = sb("PATT", [128, 194], BF16)
        IDF = sb("IDF", [128, 128], F32)
        RTS = [sb("RTS%d" % i, [128, 8], F32) for i in range(2)]
        QG = sb("QG", [128, 1], F32)
        KG = sb("KG", [128, 1], F32)
        ST = sb("ST", [128, 16], F32)
        PSB = [ps("PS%d" % i, [128, 2, 512], F32) for i in range(4)]
        PD = PSB[3][:, 0, :]
        PT = PSB[3][:, 1, :].bitcast(BF16).rearrange("p (a b) -> p a b", a=8)

        def bank(i):
            return PSB[i // 2][:, i % 2, :]

        ctr = {"f2": 0, "f4": 0, "hn": 0, "pts": 0, "st": 0, "bb": 0, "rts": 0}

        def rts():
            i = ctr["rts"] % 2
            ctr["rts"] += 1
            return RTS[i], ("RTS", i)


        def f2():
            i = ctr["f2"] % NF2
            ctr["f2"] += 1
            return F2[i], ("F2", i)

        def f4():
            i = ctr["f4"] % NF4
            ctr["f4"] += 1
            return F4[i], ("F4", i)

        def stcol():
            i = ctr["st"] % 16
            ctr["st"] += 1
            return ST[:, i:i + 1], ("ST", i)

        P.dma("pool", lambda e: e.dma_start(out=IDB[:], in_=eye), writes=["IDB"])
        P.dma("sp", lambda e: e.dma_start(out=BON[:], in_=bones), writes=["BON"])
        P.dma("sp", lambda e: e.dma_start(out=IDF[:], in_=eye), writes=["IDF"])
        P.dma("sp", lambda e: e.dma_start(out=RMT[:], in_=rmat), writes=["RMT"])
        P.op("pool", lambda e: e.memset(SEL[:], 0.0), writes=["SEL"])
        P.op("pool", lambda e: e.memset(SEL[64:65, :], 1.0), writes=["SEL"])
        P.dma("pool", lambda e: e.dma_start(out=PATT[:], in_=vpat), writes=["PATT"])
        P.op("dve", lambda e: e.tensor_copy(out=VA1[:], in_=PATT[:, :].unsqueeze(1).to_broadcast([128, 33, 194])), reads=["PATT"], writes=["VA1i"])
        P.op("dve", lambda e: e.tensor_copy(out=VBR[:].rearrange("p a b c -> p (a b) c"),
                                            in_=PATT[:, :].unsqueeze(1).to_broadcast([128, NRING * 4, 194])), reads=["PATT"], writes=["VBRi"])
        P.op("dve", lambda e: e.tensor_copy(out=VBM[:], in_=PATT[0:16, :].unsqueeze(1).to_broadcast([16, 4, 194])), reads=["PATT"], writes=["VBMi"])
        va_keys = [("VA1", t) for t in range(33)] + [("VA1b", t) for t in range(33)]
        ring_v = [("VBR", s, par) for s in range(NRING) for par in range(2)]
        for k in va_keys:
            P.last_w[k] = P.last_w["VA1i"]
        for k in ring_v:
            P.last_w[k] = P.last_w["VBRi"]
        P.last_w["VBM"] = P.last_w["VBMi"]

        groups = _groups()

        def wk(d0, ncols, k):
            return [("WIN", b, k) for b in range(d0 // 128, (d0 + ncols - 1) // 128 + 1)]

        def wka(d0, ncols):
            return [x for k in range(8) for x in wk(d0, ncols, k)]

        def load_w_cols(dst3, l, c0, ncols, d0, key, semname="w"):
            src = w_in[l].rearrange("(k p) c -> p k c", p=128)
            for k in range(8):
                P.dma("pool", lambda e, k=k: e.dma_start(out=dst3[:, k, d0:d0 + ncols], in_=src[:, k, c0:c0 + ncols]),
                      writes=wk(d0, ncols, k), key=(semname, k))

        def hsrc(l, t0, rows):
            if l == 0:
                if t0 >= SEQ:
                    return meta[0:rows, :], ()
                return x[t0:t0 + rows, :], ()
            return hbuf[t0:t0 + rows, :], [("hbuf", t0 // 128)]

        def normrope_gen(src, src_keys, gvec, gkey, dst, dst_keys, n, COS, SIN, cs_keys, bq=0, br=1):
            SQ, ksq = f2()
            P.op("act", lambda e: e.activation(out=SQ[:, 0:n], in_=src, func=AF.Square), reads=src_keys, writes=[ksq])
            yield
            pq = bank(bq)[:, 0:n]
            P.op("pe", lambda e: e.matmul(pq, lhsT=BON[:], rhs=SQ[:, 0:n], start=True, stop=True),
                 reads=[ksq, "BON"], writes=[("ps", bq)])
            yield
            RT, krt = f2()
            P.op("act", lambda e: e.activation(out=RT[:, 0:n], in_=pq, func=AF.Sqrt, bias=EPS, scale=1.0),
                 reads=[("ps", bq)], writes=[krt])
            yield
            RS, krs = f2()
            P.op("dve", lambda e: e.reciprocal(out=RS[:, 0:n], in_=RT[:, 0:n]), reads=[krt], writes=[krs])
            yield
            QN, kqn = f2()
            P.op("dve", lambda e: e.scalar_tensor_tensor(out=QN[:, 0:n], in0=src, scalar=gvec, in1=RS[:, 0:n],
                                                         op0=ALU.mult, op1=ALU.mult),
                 reads=list(src_keys) + [krs, gkey], writes=[kqn])
            yield
            pr = bank(br)[:, 0:n]
            P.op("pe", lambda e: e.matmul(pr, lhsT=RMT[:], rhs=QN[:, 0:n], start=True, stop=True),
                 reads=[kqn, "RMT"], writes=[("ps", br)])
            A, ka = f2()
            P.op("pool", lambda e: e.tensor_tensor(out=A[:, 0:n], in0=QN[:, 0:n], in1=COS[:, 0:n], op=ALU.mult),
                 reads=[kqn] + cs_keys, writes=[ka])
            yield
            B, kb = f2()
            P.op("dve", lambda e: e.tensor_tensor(out=B[:, 0:n], in0=pr, in1=SIN[:, 0:n], op=ALU.mult),
                 reads=[("ps", br)] + cs_keys, writes=[kb])
            yield
            P.op("pool", lambda e: e.tensor_tensor(out=dst, in0=A[:, 0:n], in1=B[:, 0:n], op=ALU.add),
                 reads=[ka, kb], writes=dst_keys)
            yield

        def run_gens(gens):
            gens = list(gens)
            while gens:
                for g_ in list(gens):
                    try:
                        next(g_)
                    except StopIteration:
                        gens.remove(g_)

        def normrope(*a, **k):
            run_gens([normrope_gen(*a, **k)])

        def load_cs(tok0, n):
            COS, kc, SIN, ks = COSB, "COSB", SINB, "SINB"
            P.dma("sp", lambda e: e.dma_start(out=COS[:, 0:n], in_=cosT[:, tok0:tok0 + n]), writes=[kc])
            P.dma("sp", lambda e: e.dma_start(out=SIN[:, 0:n], in_=sinT[:, tok0:tok0 + n]), writes=[ks])
            return COS, SIN, [kc, ks]

        def rms_tile(l, t0, rows, gkey="GV"):
            HT, kht = f4()
            HTf = HT[:].rearrange("p a b -> p (a b)")
            src, skeys = hsrc(l, t0, rows)
            P.dma("sp", lambda e: e.dma_start(out=HTf[0:rows, :], in_=src), reads=skeys, writes=[kht])
            i = ctr["hn"] % 2
            ctr["hn"] += 1
            HNs, khn = HN[i], ("HN", i)
            ssq, kss = stcol()
            P.op("pool", lambda e: e.memset(ssq[0:rows, :], 0.0), writes=[kss])
            P.op("act", lambda e: e.activation(out=HNs[0:rows, :], in_=HTf[0:rows, :], func=AF.Square,
                                               accum_out=ssq[0:rows, :]), reads=[kht], writes=[khn, kss])
            rt, krt = stcol()
            P.op("act", lambda e: e.activation(out=rt[0:rows, :], in_=ssq[0:rows, :], func=AF.Sqrt, bias=EPS,
                                               scale=1.0 / DM), reads=[kss], writes=[krt])
            rs, krs = stcol()
            P.op("dve", lambda e: e.reciprocal(out=rs[0:rows, :], in_=rt[0:rows, :]), reads=[krt], writes=[krs])
            P.op("dve", lambda e: e.scalar_tensor_tensor(out=HNs[0:rows, :], in0=HTf[0:rows, :], scalar=rs[0:rows, :],
                                                         in1=GV[0:rows, :], op0=ALU.mult, op1=ALU.mult),
                 reads=[kht, krs, gkey], writes=[khn])
            return HTf, kht, HNs, khn

        for l in range(depth):
            last = (l == depth - 1)
            P.phase = "A%d" % l
            P.dma("sp", lambda e, l=l: e.dma_start(out=GV[:], in_=ng[l]), writes=["GV"])
            P.dma("sp", lambda e, l=l: e.dma_start(out=KG[:], in_=kg[l]), writes=["KG"])
            P.dma("sp", lambda e, l=l: e.dma_start(out=QG[:], in_=qg[l]), writes=["QG"])
            load_w_cols(WIN, l, C_KA, 256, 768, "WIN")
            load_w_cols(WIN, l, C_KB, 1024, 1024, "WIN")
            for c in range(4):
                P.dma("pool", lambda e, c=c, l=l: e.dma_start(out=W2[0:64, c, :], in_=w_oa[l, c * 64:(c + 1) * 64, :]), writes=[("W2", c)], key=("w2", c))
                P.dma("pool", lambda e, c=c, l=l: e.dma_start(out=W2[64:128, c, :], in_=w_oa[l, (4 + c) * 64:(5 + c) * 64, :]), writes=[("W2", c)], key=("w2", c))
                P.dma("pool", lambda e, c=c, l=l: e.dma_start(out=W2[:, 4 + c, :], in_=w_ob[l, c * 128:(c + 1) * 128, :]), writes=[("W2", 4 + c)], key=("w2", 4 + c))
            for c in range(4):
                load_w_cols(WIN, l, C_QA + 64 * c, 64, c * 128, "WIN", semname="wp")
                load_w_cols(WIN, l, C_QA + 256 + 64 * c, 64, c * 128 + 64, "WIN", semname="wp")
            for c in range(2):
                load_w_cols(WIN, l, C_ZA + 64 * c, 64, 512 + c * 128, "WIN", semname="wp")
                load_w_cols(WIN, l, C_ZA + 256 + 64 * c, 64, 512 + c * 128 + 64, "WIN", semname="wp")
            HB = [HNT, OA]

            def hk(bf, i):
                return ("HNT", i) if bf == 0 else ("OAH", i)

            oa_alias = [("OA", x_) for x_ in range(8)] + [("OAH", i_) for i_ in range(4)]
            P.op("pool", lambda e: e.memset(DMY[0:1, 0:1], 0.0), writes=oa_alias)

            def stage1_gen(grp, bf):
                tok0, n, tiles = grp
                HBb = HB[bf]

                def evac(i, rows):
                    if i % 2 == 0:
                        P.op("act", lambda e: e.copy(out=HBb[:, :, i * 128:i * 128 + rows], in_=PT[:, :, 0:rows]),
                             reads=[("ps", 7)], writes=[hk(bf, i)])
                    else:
                        P.op("dve", lambda e: e.tensor_copy(out=HBb[:, :, i * 128:i * 128 + rows], in_=PT[:, :, 0:rows]),
                             reads=[("ps", 7)], writes=[hk(bf, i)])

                prev_t = None
                for i, (t0, rows) in enumerate(tiles):
                    HTf, kht, HNs, khn = rms_tile(l, t0, rows)
                    yield
                    if prev_t is not None:
                        evac(*prev_t)
                    for k in range(8):
                        P.op("pe", lambda e, k=k, HNs=HNs, rows=rows: e.transpose(PT[:, k, 0:rows], HNs[0:rows, k * 128:(k + 1) * 128],
                                                                                  IDB[0:rows, 0:rows]),
                             reads=[khn, "IDB"], writes=[("ps", 7)])
                    prev_t = (i, rows)
                    yield
                evac(*prev_t)
                yield

            def proj_gen(grp, bf):
                tok0, n, tiles = grp
                HBb = HB[bf]
                hkeys = [hk(bf, i) for i in range(len(tiles))]
                tkeys = [t0 // 128 for (t0, _) in tiles]
                COS, SIN, cskeys = load_cs(tok0, n)
                P.dma("pool", lambda e: e.dma_start(out=hnts[:, :, tok0:tok0 + n], in_=HBb[:, :, 0:n]),
                      reads=hkeys, writes=[("hnts", t) for t in tkeys], key=("HNTst", bf))
                pk = bank(6)[:, 0:n]
                for k in range(8):
                    P.op("pe", lambda e, k=k: e.matmul(pk, lhsT=WIN[:, k, 768:896], rhs=HBb[:, k, 0:n], start=(k == 0), stop=(k == 7)),
                         reads=hkeys + wk(768, 128, k), writes=[("ps", 6)])
                yield
                nr = normrope_gen(pk, [("ps", 6)], KG[:, 0:1], "KG", KAT[:, tok0:tok0 + n], [("KAT", t) for t in tkeys], n, COS, SIN, cskeys)
                for i, (t0, rows) in enumerate(tiles):
                    pv = bank(2)[0:rows, 0:128]
                    for k in range(8):
                        P.op("pe", lambda e, k=k, i=i, rows=rows, pv=pv: e.matmul(pv, lhsT=HBb[:, k, i * 128:i * 128 + rows], rhs=WIN[:, k, 896:1024],
                                                                               start=(k == 0), stop=(k == 7)),
                             reads=[hk(bf, i)] + wk(896, 128, k), writes=[("ps", 2)])
                    tt = t0 // 128
                    P.op("act", lambda e, tt=tt, rows=rows, pv=pv: e.copy(out=VA1[0:rows, tt, 0:64], in_=pv[:, 0:64]),
                         reads=[("ps", 2)], writes=[("VA1", tt), "ps2rd"])
                    P.op("dve", lambda e, tt=tt, rows=rows, pv=pv: e.tensor_copy(out=VA1[0:rows, tt, 130:194], in_=pv[:, 64:128]),
                         reads=[("ps", 2)], writes=[("VA1b", tt), "ps2rd"])
                    nr = step_gen(nr)
                    yield
                for c in range(4):
                    bk = 3 + (c % 2)
                    pb = bank(bk)[:, 0:n]
                    for k in range(8):
                        P.op("pe", lambda e, k=k, c=c, pb=pb: e.matmul(pb, lhsT=WIN[:, k, 1024 + c * 128:1024 + (c + 1) * 128], rhs=HBb[:, k, 0:n],
                                                                      start=(k == 0), stop=(k == 7)),
                             reads=hkeys + wk(1024 + c * 128, 128, k), writes=[("ps", bk)])
                    P.op("dve", lambda e, c=c, pb=pb: e.tensor_copy(out=QAT[:, c, 0:n], in_=pb), reads=[("ps", bk)], writes=[("QAT", c)])
                    nr = step_gen(nr)
                    yield
                P.dma("pool", lambda e: e.dma_start(out=kbs[:, :, tok0:tok0 + n], in_=QAT[:, :, 0:n]),
                      reads=[("QAT", c) for c in range(4)], writes=[("kbs", t) for t in tkeys], key=("KBst",))
                for i, (t0, rows) in enumerate(tiles):
                    bk = 5 if i % 2 == 0 else 2
                    pv = bank(bk)[0:rows, :]
                    for k in range(8):
                        P.op("pe", lambda e, k=k, i=i, rows=rows, pv=pv: e.matmul(pv, lhsT=HBb[:, k, i * 128:i * 128 + rows], rhs=WIN[:, k, 1536:2048],
                                                                               start=(k == 0), stop=(k == 7)),
                             reads=[hk(bf, i)] + wk(1536, 512, k), writes=[("ps", bk)])
                    P.op("act", lambda e, i=i, rows=rows, pv=pv: e.copy(out=QBT[0:rows, i, :], in_=pv), reads=[("ps", bk)], writes=[("QBT", i)])
                    P.dma("pool", lambda e, i=i, rows=rows, t0=t0: e.dma_start(out=vbs[t0:t0 + rows, :], in_=QBT[0:rows, i, :]),
                          reads=[("QBT", i)], writes=[("vbs", t0 // 128)], key=("VBst", i))
                    nr = step_gen(nr)
                    yield
                while nr is not None:
                    nr = step_gen(nr)
                    yield

            def step_gen(g_):
                if g_ is None:
                    return None
                try:
                    next(g_)
                    return g_
                except StopIteration:
                    return None

            s1 = stage1_gen(groups[0], 0)
            while s1 is not None:
                s1 = step_gen(s1)
            for gi_, grp in enumerate(groups):
                bf = gi_ % 2
                pg = proj_gen(grp, bf)
                s1 = stage1_gen(groups[gi_ + 1], 1 - bf) if gi_ + 1 < len(groups) else None
                while s1 is not None or pg is not None:
                    s1 = step_gen(s1)
                    for _ in range(3):
                        pg = step_gen(pg)
            P.op("pool", lambda e: e.memset(DMY[0:1, 0:1], 0.0), writes=oa_alias)

            if last:
                b_groups = groups[:8]
            else:
                b_groups = groups
            P.phase = "B1w%d" % l
            load_w_cols(WIN, l, C_QB, 512, 1024, "WIN")
            for c in range(2, 4):
                load_w_cols(WIN, l, C_ZA + 64 * c, 64, 512 + c * 128, "WIN")
                load_w_cols(WIN, l, C_ZA + 256 + 64 * c, 64, 512 + c * 128 + 64, "WIN")
            load_w_cols(WIN, l, C_ZB, 512, 1536, "WIN")

            def build_bias(dst, dkey, t, cidx, l=l):
                for rk_l in range(2):
                    for rq_l in range(2):
                        rk = 2 * cidx + rk_l
                        rq = 2 * t + rq_l
                        rs = min(max(rq - 4, 0), 56)
                        idx = (rk - rq + 7) if (rs <= rk <= rs + 7) else 15
                        P.dma("pool", lambda e, rk_l=rk_l, rq_l=rq_l, idx=idx: e.dma_start(
                            out=dst[rk_l * 64:(rk_l + 1) * 64, :, :, rq_l * 64:(rq_l + 1) * 64], in_=mtab[l, idx]),
                            writes=[dkey], key=(dkey, rk_l, rq_l) if dkey[0] == "BB" else ("bias", rk_l, rq_l))

            for j in range(5):
                build_bias(BI[:, j], ("BI", j), 10, 10 + j - 2)
            P.dma("sp", lambda e: e.dma_start(out=KBM[:], in_=kbs[:, :, SEQ:LT]), reads=[("kbs", 32)], writes=["KBM"])
            for par in range(2):
                P.dma("sp", lambda e, par=par: e.dma_start(out=VBM[:, :, 130 * par:130 * par + 64],
                                                            in_=vbs[SEQ:LT, :].rearrange("p (a t d) -> p a t d", a=4, t=2)[:, :, par, :]),
                      reads=[("vbs", 32)], writes=["VBM"], key=("VBM", par))

            def ring_load(cidx):
                s = cidx % NRING
                P.dma("sp", lambda e: e.dma_start(out=KBR[:, :, s * 128:(s + 1) * 128], in_=kbs[:, :, cidx * 128:(cidx + 1) * 128]),
                      reads=[("kbs", cidx)], writes=[("KBR", s)])
                for par in range(2):
                    P.dma("sp", lambda e, par=par: e.dma_start(out=VBR[:, s, :, 130 * par:130 * par + 64],
                                                                in_=vbs[cidx * 128:(cidx + 1) * 128, :].rearrange("p (a t d) -> p a t d", a=4, t=2)[:, :, par, :]),
                          reads=[("vbs", cidx)], writes=[("VBR", s, par)], key=("VBR", s, par))

            ring_next = 0

            def pair_finish(zmm, zper, ncols, blocks, out_fn, out_keys):
                P4, P5, P6 = bank(4), bank(5), bank(6)
                X4, kx4 = f2()
                P.op("act", lambda e: e.copy(out=X4[0:65, 0:ncols], in_=P4[0:65, 0:ncols]), reads=[("ps", 4)], writes=[kx4])
                X5, kx5 = f2()
                P.op("dve", lambda e: e.tensor_copy(out=X5[:, 0:ncols], in_=P5[:, 0:ncols]), reads=[("ps", 5)], writes=[kx5])
                yield
                for zi, zf in enumerate(zmm):
                    zf()
                    if zi % zper == zper - 1:
                        yield
                TH, kth = f2()
                P.op("act", lambda e: e.activation(out=TH[:, 0:ncols], in_=P6[:, 0:ncols], func=AF.Tanh, scale=0.5), reads=[("ps", 6)], writes=[kth])
                ZS, kzs = f2()
                P.op("dve", lambda e: e.scalar_tensor_tensor(out=ZS[:, 0:ncols], in0=TH[:, 0:ncols], scalar=1.0, in1=P6[:, 0:ncols],
                                                             op0=ALU.add, op1=ALU.mult), reads=[kth, ("ps", 6)], writes=[kzs])
                T1, kt1 = f2()
                P.op("dve", lambda e: e.scalar_tensor_tensor(out=T1[0:64, 0:ncols], in0=X4[0:64, 0:ncols], scalar=0.5, in1=ZS[0:64, 0:ncols],
                                                             op0=ALU.mult, op1=ALU.mult), reads=[kx4, kzs], writes=[kt1])
                P.op("dve", lambda e: e.scalar_tensor_tensor(out=T1[64:128, 0:ncols], in0=X5[64:128, 0:ncols], scalar=0.5, in1=ZS[64:128, 0:ncols],
                                                             op0=ALU.mult, op1=ALU.mult), reads=[kx5, kzs, kt1], writes=[kt1])
                yield
                NB = len(blocks)
                for bi_, (c0, w) in enumerate(blocks):
                    P.op("pe", lambda e, bi_=bi_, c0=c0, w=w: e.matmul(P6[0:w, bi_:bi_ + 1], lhsT=X4[0:65, c0:c0 + w], rhs=SEL[0:65, 0:1],
                                                                      start=True, stop=True), reads=[kx4, "SEL"], writes=[("ps", 6)])
                    P.op("pe", lambda e, bi_=bi_, c0=c0, w=w: e.matmul(P6[0:w, NB + bi_:NB + bi_ + 1], lhsT=X5[0:65, c0:c0 + w], rhs=IDF[0:65, 0:1],
                                                                      start=False, stop=True, skip_group_check=True), reads=[kx5, "IDF"], writes=[("ps", 6)])
                yield
                RT, krt = rts()
                P.op("dve", lambda e: e.reciprocal(out=RT[:, 0:2 * NB], in_=P6[:, 0:2 * NB]), reads=[("ps", 6)], writes=[krt])
                RTB, krb = f2()
                RTBv = RTB[:, :].rearrange("p (a b) -> p a b", a=4)
                P.op("dve", lambda e: e.tensor_copy(out=RTBv[:, 0:NB, 0:64], in_=RT[:, 0:NB].unsqueeze(2).to_broadcast([128, NB, 64])),
                     reads=[krt], writes=[krb])
                P.op("dve", lambda e: e.tensor_copy(out=RTBv[:, 0:NB, 64:128], in_=RT[:, NB:2 * NB].unsqueeze(2).to_broadcast([128, NB, 64])),
                     reads=[krt, krb], writes=[krb])
                for bi_, (c0, w) in enumerate(blocks):
                    P.op("pe", lambda e, bi_=bi_, c0=c0, w=w: e.matmul(P6[:, c0:c0 + w], lhsT=RTBv[0:w, bi_, :], rhs=IDF[0:w, 0:w],
                                                                      start=(bi_ == 0), stop=True, skip_group_check=True),
                         reads=[krb, "IDF"], writes=[("ps", 6)])
                yield
                P.op("dve", lambda e: out_fn(e, T1, P6), reads=[kt1, ("ps", 6)], writes=out_keys)
                yield

            def step_gen(g_):
                if g_ is None:
                    return None
                try:
                    next(g_)
                    return g_
                except StopIteration:
                    return None

            def drain_gen(g_):
                while g_ is not None:
                    g_ = step_gen(g_)

            def bg_outproj(which, tok0p, np_, tkp, rkeys_fn):
                SRC = OA if which == 0 else OB
                dst = yas if which == 0 else ybs
                dname = "yas" if which == 0 else "ybs"
                for dc in range(8):
                    py = bank(7)[:, 0:np_]
                    for c in range(4):
                        P.op("pe", lambda e, c=c, dc=dc: e.matmul(py, lhsT=W2[:, 4 * which + c, dc * 128:(dc + 1) * 128], rhs=SRC[:, c, 0:np_],
                                                                 start=(c == 0), stop=(c == 3)),
                             reads=rkeys_fn(c) + [("W2", 4 * which + c)], writes=[("ps", 7)])
                        if c % 2 == 1:
                            yield
                    Y, ky = f2()
                    P.op("dve", lambda e, Y=Y: e.tensor_copy(out=Y[:, 0:np_], in_=py), reads=[("ps", 7)], writes=[ky])
                    P.dma("pool", lambda e, dc=dc, Y=Y: e.dma_start(out=dst[:, dc, tok0p:tok0p + np_], in_=Y[:, 0:np_]),
                          reads=[ky], writes=[(dname, dc, t) for t in tkp], key=ky)
                    yield
                    yield

            def bg_qB(n_, hk):
                for c in range(4):
                    pk = bank(7)[:, 0:n_]
                    for k in range(8):
                        P.op("pe", lambda e, k=k, c=c: e.matmul(pk, lhsT=WIN[:, k, 1024 + c * 128:1024 + (c + 1) * 128], rhs=HNT[:, k, 0:n_],
                                                               start=(k == 0), stop=(k == 7)),
                             reads=hk + wk(1024 + c * 128, 128, k), writes=[("ps", 7)])
                        if k % 2 == 1:
                            yield
                    P.op("dve", lambda e, c=c: e.tensor_scalar(out=QBT[:, c, 0:n_], in0=pk, scalar1=0.125, scalar2=None, op0=ALU.mult),
                         reads=[("ps", 7)], writes=[("QBT", c)])
                    yield
                    yield

            def chain_gens(*gs):
                for g_ in gs:
                    if g_ is not None:
                        yield from g_

            prev_bg = []

            for (tok0, n, tiles) in b_groups:
                P.dma("sp", lambda e, tok0=tok0, n=n: e.dma_start(out=HNT[:, :, 0:n], in_=hnts[:, :, tok0:tok0 + n]),
                      reads=[("hnts", t0 // 128) for (t0, _) in tiles], writes=[("HNT", i) for i in range(4)], key=("HNTld",))
                hkeys = [("HNT", i) for i in range(4)]
                COS, SIN, cskeys = load_cs(tok0, n)
                P.phase = "qA%d" % l
                for c0 in (0, 2):
                    gens = []
                    for ci_, c in enumerate((c0, c0 + 1)):
                        bk = 6 + ci_
                        pk = bank(bk)[:, 0:n]
                        for k in range(8):
                            P.op("pe", lambda e, k=k, c=c, pk=pk, n=n: e.matmul(pk, lhsT=WIN[:, k, c * 128:(c + 1) * 128], rhs=HNT[:, k, 0:n],
                                                                       start=(k == 0), stop=(k == 7)),
                                 reads=hkeys + wk(c * 128, 128, k), writes=[("ps", bk)])
                        gens.append(normrope_gen(pk, [("ps", bk)], QG[:, 0:1], "QG", QAT[:, c, 0:n], [("QAT", c)], n, COS, SIN, cskeys,
                                                 bq=2 * ci_, br=2 * ci_ + 1))
                    run_gens(gens)
                P.phase = "GQA%d" % l
                P.phase = "GQA%d" % l
                PC = PSB[2]
                pending = []

                def gqa_S(c, kt, n=n):
                    k0 = kt * 128
                    kn = 128 if kt < 32 else NMETA
                    PSs = PSB[kt % 2]
                    skeys = [("ps", 2 * (kt % 2)), ("ps", 2 * (kt % 2) + 1)]
                    for j in range(2):
                        P.op("pe", lambda e, j=j: e.matmul(
                            PSs[0:kn, j, 0:n], lhsT=KAT[64 * j:64 * j + 64, k0:k0 + kn], rhs=QAT[64 * j:64 * j + 64, c, 0:n], start=True, stop=True),
                            reads=[("KAT", kt), ("QAT", c)], writes=[skeys[j]])

                def gqa_E(c, kt, n=n):
                    kn = 128 if kt < 32 else NMETA
                    PSs = PSB[kt % 2]
                    skeys = [("ps", 2 * (kt % 2)), ("ps", 2 * (kt % 2) + 1)]
                    pi = ctr["pts"] % 3
                    ctr["pts"] += 1
                    PTs, kpt = PTS[pi], ("PTS", pi)
                    P.op("act", lambda e: e.activation(out=PTs[0:kn, :, 0:n], in_=PSs[0:kn, :, 0:n], func=AF.Exp, scale=0.125),
                         reads=skeys, writes=[kpt])
                    return PTs, kpt

                def gqa_PV(c, kt, PTs, kpt, n=n):
                    kn = 128 if kt < 32 else NMETA
                    P.op("pe", lambda e: e.matmul(bank(4)[0:65, 0:n], lhsT=VA1[0:kn, kt, 0:65], rhs=PTs[0:kn, 0, 0:n], start=(kt == 0), stop=(kt == 32)),
                         reads=[("VA1", kt), kpt], writes=[("ps", 4)])
                    P.op("pe", lambda e: e.matmul(bank(5)[:, 0:n], lhsT=VA1[0:kn, kt, 66:194], rhs=PTs[0:kn, 1, 0:n], start=(kt == 0), stop=(kt == 32)),
                         reads=[("VA1b", kt), kpt], writes=[("ps", 5)])

                def gqa_finish(c, n=n):
                    P6 = bank(6)
                    zmm = []
                    for k in range(8):
                        zmm.append(lambda k=k: P.op("pe", lambda e: e.matmul(P6[:, 0:n], lhsT=WIN[:, k, 512 + c * 128:512 + (c + 1) * 128], rhs=HNT[:, k, 0:n],
                                                                            start=(k == 0), stop=(k == 7)),
                                                    reads=hkeys + wk(512 + c * 128, 128, k), writes=[("ps", 6)]))
                    nb = (n + 127) // 128
                    blocks = [(qb * 128, min(128, n - qb * 128)) for qb in range(nb)]
                    return pair_finish(zmm, 2, n, blocks,
                                       lambda e, T1, BC: e.tensor_tensor(out=OA[:, c, 0:n], in0=T1[:, 0:n], in1=BC[:, 0:n], op=ALU.mult),
                                       [("OA", c)])

                bg_rest = chain_gens(prev_bg[1] if prev_bg else None, bg_qB(n, hkeys))
                prev_bg = []
                fin = None
                for c in range(4):
                    gqa_S(c, 0)
                    gqa_S(c, 1)
                    fin = step_gen(fin)
                    for kt in range(33):
                        PTs_, kpt_ = gqa_E(c, kt)
                        if kt + 2 < 33:
                            gqa_S(c, kt + 2)
                        gqa_PV(c, kt, PTs_, kpt_)
                        fin = step_gen(fin)
                        bg_rest = step_gen(bg_rest)
                    drain_gen(fin)
                    fin = gqa_finish(c)
                drain_gen(fin)
                drain_gen(bg_rest)
                if (tok0, n) == (b_groups[-1][0], b_groups[-1][1]):
                    load_w_cols(WIN, l, C_GA, 512, 0, "WIN", semname="wp")
                    load_w_cols(WIN, l, C_GB, 512, 1024, "WIN", semname="wp")
                    load_w_cols(WIN, l, C_GA + 512, 512, 512, "WIN", semname="wp")
                tkeys = [t0 // 128 for (t0, _) in tiles]
                P.phase = "NA%d" % l
                fin = None
                bg_na = bg_outproj(0, tok0, n, tkeys, lambda c_: [("OA", c_)])
                for i, (t0, rows) in enumerate(tiles):
                    is_meta = (t0 >= SEQ)
                    t = t0 // 128
                    chunks = []
                    interior = False
                    if not is_meta:
                        r0, r1 = 2 * t, 2 * t + 1
                        lo = min(max(r0 - 4, 0), 56) // 2
                        hi = (min(max(r1 - 4, 0), 56) + 7) // 2
                        want = min(max(t + 3, hi), 31)
                        while ring_next <= want:
                            ring_load(ring_next)
                            ring_next += 1
                        interior = (2 <= t <= 29)
                        chunks = list(range(lo, hi + 1))
                    nch = len(chunks) + 1
                    q0 = i * 128

                    def na_S(ci, chunks=chunks, t=t, rows=rows, q0=q0, interior=interior):
                        PSs = PSB[ci % 2]
                        skeys = [("ps", 2 * (ci % 2)), ("ps", 2 * (ci % 2) + 1)]
                        if ci < len(chunks):
                            cidx = chunks[ci]
                            s_ = cidx % NRING
                            if interior:
                                BT, kbt = BI[:, cidx - t + 2], ("BI", cidx - t + 2)
                            else:
                                bi = ctr["bb"] % 2
                                ctr["bb"] += 1
                                BT, kbt = BB[:, bi], ("BB", bi)
                                build_bias(BT, kbt, t, cidx)
                            for par in range(2):
                                P.op("pe", lambda e, par=par: e.matmul(PSs[:, par, :], lhsT=IDB[:], rhs=BT[:, par].rearrange("p a b -> p (a b)"),
                                                                       start=True, stop=False, skip_group_check=True),
                                     reads=[kbt, "IDB"], writes=[skeys[par]])
                            for p4 in range(4):
                                for par in range(2):
                                    P.op("pe", lambda e, par=par, p4=p4: e.matmul(
                                        PSs[:, par, p4 * 128:p4 * 128 + rows], lhsT=KBR[64 * par:64 * par + 64, p4, s_ * 128:(s_ + 1) * 128],
                                        rhs=QBT[64 * par:64 * par + 64, p4, q0:q0 + rows], start=False, stop=True, skip_group_check=True),
                                        reads=[("KBR", s_), ("QBT", p4)], writes=[skeys[par]])
                        else:
                            for p4 in range(4):
                                for par in range(2):
                                    P.op("pe", lambda e, par=par, p4=p4: e.matmul(
                                        PSs[0:16, par, p4 * 128:p4 * 128 + rows], lhsT=KBM[64 * par:64 * par + 64, p4, :],
                                        rhs=QBT[64 * par:64 * par + 64, p4, q0:q0 + rows], start=(p4 == 0), stop=True, skip_group_check=True),
                                        reads=["KBM", ("QBT", p4)], writes=[skeys[par]])

                    def na_E(ci, chunks=chunks, rows=rows):
                        PSs = PSB[ci % 2]
                        skeys = [("ps", 2 * (ci % 2)), ("ps", 2 * (ci % 2) + 1)]
                        pi = ctr["pts"] % 3
                        ctr["pts"] += 1
                        PTs, kpt = PTS[pi], ("PTS", pi)
                        if ci != len(chunks):
                            P.op("act", lambda e: e.activation(out=PTs[:, :, :], in_=PSs[:, :, :], func=AF.Exp), reads=skeys, writes=[kpt])
                        else:
                            P.op("act", lambda e: e.activation(
                                out=PTs[0:16].rearrange("p a (b c) -> p a b c", b=4)[:, :, :, 0:rows],
                                in_=PSs[0:16].rearrange("p a (b c) -> p a b c", b=4)[:, :, :, 0:rows], func=AF.Exp),
                                reads=skeys, writes=[kpt])
                        return PTs, kpt

                    def na_PV(ci, PTs, kpt, chunks=chunks, rows=rows):
                        last_c = (ci == len(chunks))
                        s_ = (chunks[ci] % NRING) if not last_c else 0
                        for p4 in range(4):
                            for par in range(2):
                                if par == 0:
                                    outp = bank(4)[0:65, p4 * 128:p4 * 128 + rows]
                                    lo, hi = 0, 65
                                else:
                                    outp = bank(5)[:, p4 * 128:p4 * 128 + rows]
                                    lo, hi = 66, 194
                                if not last_c:
                                    P.op("pe", lambda e, par=par, p4=p4, outp=outp, lo=lo, hi=hi: e.matmul(
                                        outp, lhsT=VBR[:, s_, p4, lo:hi], rhs=PTs[:, par, p4 * 128:p4 * 128 + rows],
                                        start=(ci == 0 and p4 == 0), stop=False, skip_group_check=True),
                                        reads=[("VBR", s_, par), kpt], writes=[("ps", 4 + par)])
                                else:
                                    P.op("pe", lambda e, par=par, p4=p4, outp=outp, lo=lo, hi=hi: e.matmul(
                                        outp, lhsT=VBM[0:16, p4, lo:hi], rhs=PTs[0:16, par, p4 * 128:p4 * 128 + rows],
                                        start=(ci == 0 and p4 == 0), stop=True, skip_group_check=True),
                                        reads=["VBM", kpt], writes=[("ps", 4 + par)])

                    def na_finish(i=i, rows=rows, q0=q0):
                        P6 = bank(6)
                        zmm = []
                        for p4 in range(4):
                            for k in range(8):
                                zmm.append(lambda p4=p4, k=k: P.op("pe", lambda e: e.matmul(
                                    P6[:, p4 * 128:p4 * 128 + rows], lhsT=WIN[:, k, 1536 + p4 * 128:1536 + (p4 + 1) * 128],
                                    rhs=HNT[:, k, q0:q0 + rows], start=(k == 0), stop=(k == 7), skip_group_check=True),
                                    reads=[("HNT", i)] + wk(1536 + p4 * 128, 128, k), writes=[("ps", 6)]))
                        blocks = [(p4 * 128, rows) for p4 in range(4)]
                        return pair_finish(zmm, 16, 512, blocks,
                                           lambda e, T1, BC: e.tensor_tensor(
                                               out=OB[:, :, q0:q0 + rows], in0=T1[:, :].rearrange("p (a b) -> p a b", a=4)[:, :, 0:rows],
                                               in1=BC[:, :].rearrange("p (a b) -> p a b", a=4)[:, :, 0:rows], op=ALU.mult),
                                           [("OB", i)])

                    na_S(0)
                    if nch > 1:
                        na_S(1)
                    fin = step_gen(fin)
                    for ci in range(nch):
                        PTs_, kpt_ = na_E(ci)
                        if ci + 2 < nch:
                            na_S(ci + 2)
                        na_PV(ci, PTs_, kpt_)
                        fin = step_gen(fin)
                        bg_na = step_gen(bg_na)
                        bg_na = step_gen(bg_na)
                    drain_gen(fin)
                    fin = na_finish()
                drain_gen(fin)
                drain_gen(bg_na)
                obk = [("OB", i_) for i_ in range(len(tiles))]
                prev_bg = [None, bg_outproj(1, tok0, n, tkeys, lambda c_, obk=obk: list(obk))]
            P.phase = "yflush%d" % l
            for g_ in prev_bg:
                drain_gen(g_)
            prev_bg = []

            P.phase = "B2%d" % l
            load_w_cols(WIN, l, C_GB + 512, 512, 1536, "WIN")
            for k in range(8):
                P.dma("pool", lambda e, k=k, l=l: e.dma_start(out=W2[:, k, :], in_=w_out[l, k * 128:(k + 1) * 128, :]), writes=[("W2", k)], key=("w2", k))
            if last:
                P.dma("sp", lambda e: e.dma_start(out=GV[:], in_=fg), writes=["GV"])
            MIX = OA
            for (tok0, n, tiles) in b_groups:
                tkeys = [t0 // 128 for (t0, _) in tiles]
                P.dma("sp", lambda e, tok0=tok0, n=n: e.dma_start(out=HNT[:, :, 0:n], in_=hnts[:, :, tok0:tok0 + n]),
                      reads=[("hnts", t) for t in tkeys], writes=[("HNT", i) for i in range(4)], key=("HNTld",))
                hkeys = [("HNT", i) for i in range(4)]
                for dc in range(8):
                    YA, kya = f2()
                    YB, kyb = f2()
                    P.dma("sp", lambda e, dc=dc, tok0=tok0, n=n, YA=YA: e.dma_start(out=YA[:, 0:n], in_=yas[:, dc, tok0:tok0 + n]),
                          reads=[("yas", dc, t) for t in tkeys], writes=[kya])
                    P.dma("sp", lambda e, dc=dc, tok0=tok0, n=n, YB=YB: e.dma_start(out=YB[:, 0:n], in_=ybs[:, dc, tok0:tok0 + n]),
                          reads=[("ybs", dc, t) for t in tkeys], writes=[kyb])
                    ba, bb_ = (0, 1) if dc % 2 == 0 else (2, 3)
                    pa = bank(ba)[:, 0:n]
                    pb = bank(bb_)[:, 0:n]
                    for k in range(8):
                        P.op("pe", lambda e, k=k, dc=dc, n=n, pa=pa: e.matmul(pa, lhsT=WIN[:, k, dc * 128:(dc + 1) * 128], rhs=HNT[:, k, 0:n],
                                                                           start=(k == 0), stop=(k == 7)),
                             reads=hkeys + wk(dc * 128, 128, k), writes=[("ps", ba)])
                    for k in range(8):
                        P.op("pe", lambda e, k=k, dc=dc, n=n, pb=pb: e.matmul(pb, lhsT=WIN[:, k, 1024 + dc * 128:1024 + (dc + 1) * 128], rhs=HNT[:, k, 0:n],
                                                                           start=(k == 0), stop=(k == 7)),
                             reads=hkeys + wk(1024 + dc * 128, 128, k), writes=[("ps", bb_)])
                    TA, kta = f2()
                    TB, ktb = f2()
                    P.op("act", lambda e, n=n, pa=pa, TA=TA: e.activation(out=TA[:, 0:n], in_=pa, func=AF.Tanh, scale=0.5), reads=[("ps", ba)], writes=[kta])
                    P.op("act", lambda e, n=n, pb=pb, TB=TB: e.activation(out=TB[:, 0:n], in_=pb, func=AF.Tanh, scale=0.5), reads=[("ps", bb_)], writes=[ktb])
                    P.op("dve", lambda e, n=n, TA=TA, YA=YA: e.scalar_tensor_tensor(out=TA[:, 0:n], in0=TA[:, 0:n], scalar=1.0, in1=YA[:, 0:n],
                                                                                  op0=ALU.add, op1=ALU.mult), reads=[kta, kya], writes=[kta])
                    P.op("dve", lambda e, n=n, TB=TB, YB=YB: e.scalar_tensor_tensor(out=TB[:, 0:n], in0=TB[:, 0:n], scalar=1.0, in1=YB[:, 0:n],
                                                                                  op0=ALU.add, op1=ALU.mult), reads=[ktb, kyb], writes=[ktb])
                    P.op("pool", lambda e, dc=dc, n=n, TA=TA, TB=TB: e.tensor_tensor(out=MIX[:, dc, 0:n], in0=TA[:, 0:n], in1=TB[:, 0:n], op=ALU.add),
                         reads=[kta, ktb], writes=[("OA", dc)])
                mkeys = [("OA", dc) for dc in range(8)]
                for i, (t0, rows) in enumerate(tiles):
                    PY = PSB[2] if i % 2 == 0 else PSB[3]
                    pb0 = 4 if i % 2 == 0 else 6
                    HT, kht = f4()
                    HTf = HT[:].rearrange("p a b -> p (a b)")
                    src, skeys = hsrc(l, t0, rows)
                    P.dma("sp", lambda e, rows=rows, src=src, HTf=HTf: e.dma_start(out=HTf[0:rows, :], in_=src), reads=skeys, writes=[kht])
                    for half in range(2):
                        for k in range(8):
                            P.op("pe", lambda e, half=half, k=k, i=i, rows=rows, PY=PY: e.matmul(
                                PY[0:rows, half, :], lhsT=MIX[:, k, i * 128:i * 128 + rows], rhs=W2[:, k, half * 512:(half + 1) * 512],
                                start=(k == 0), stop=(k == 7)), reads=mkeys + [("W2", k)], writes=[("ps", pb0 + half)])
                    HNW, khw = f4()
                    P.op("dve", lambda e, rows=rows, HNW=HNW, HT=HT, PY=PY: e.scalar_tensor_tensor(out=HNW[0:rows], in0=PY[0:rows], scalar=0.5, in1=HT[0:rows],
                                                                                                 op0=ALU.mult, op1=ALU.add),
                         reads=[("ps", pb0), ("ps", pb0 + 1), kht], writes=[khw])
                    HNWf = HNW[:].rearrange("p a b -> p (a b)")
                    if not last:
                        P.dma("pool", lambda e, rows=rows, t0=t0, HNWf=HNWf: e.dma_start(out=hbuf[t0:t0 + rows, :], in_=HNWf[0:rows, :]),
                              reads=[khw], writes=[("hbuf", t0 // 128)], key=khw)
                    else:
                        OT, kot = f4()
                        OTf = OT[:].rearrange("p a b -> p (a b)")
                        ssq, kss = stcol()
                        P.op("pool", lambda e, rows=rows, ssq=ssq: e.memset(ssq[0:rows, :], 0.0), writes=[kss])
                        P.op("act", lambda e, rows=rows, OTf=OTf, HNWf=HNWf, ssq=ssq: e.activation(out=OTf[0:rows, :], in_=HNWf[0:rows, :], func=AF.Square,
                                                                                                   accum_out=ssq[0:rows, :]), reads=[khw], writes=[kot, kss])
                        rt, krt = stcol()
                        P.op("act", lambda e, rows=rows, rt=rt, ssq=ssq: e.activation(out=rt[0:rows, :], in_=ssq[0:rows, :], func=AF.Sqrt, bias=EPS, scale=1.0 / DM),
                             reads=[kss], writes=[krt])
                        rs, krs = stcol()
                        P.op("dve", lambda e, rows=rows, rs=rs, rt=rt: e.reciprocal(out=rs[0:rows, :], in_=rt[0:rows, :]), reads=[krt], writes=[krs])
                        P.op("dve", lambda e, rows=rows, OTf=OTf, HNWf=HNWf, rs=rs: e.scalar_tensor_tensor(
                            out=OTf[0:rows, :], in0=HNWf[0:rows, :], scalar=rs[0:rows, :], in1=GV[0:rows, :], op0=ALU.mult, op1=ALU.mult),
                            reads=[khw, krs, "GV"], writes=[kot])
                        P.dma("pool", lambda e, rows=rows, t0=t0, OTf=OTf: e.dma_start(out=out[t0:t0 + rows, :], in_=OTf[0:rows, :]),
                              reads=[kot], writes=[("out", t0 // 128)], key=kot)
        P.emit(nc)
    return nc, P


def _host_consts(na_rpb):
    W = 64
    c = np.arange(W)
    cs = np.clip(c - 8, 0, W - 16)
    ck = c[:, None]
    cq = c[None, :]
    valid = (ck >= cs[None, :]) & (ck < cs[None, :] + 16)
    bidx = np.clip(ck - cq + 15, 0, 30)
    mt = np.full((DEPTH, 16, 64, 8, 64), NEG, np.float32)
    for l in range(DEPTH):
        for i in range(15):
            g = na_rpb[l][:, i, :][:, bidx]
            g = np.where(valid[None], g, np.float32(NEG))
            mt[l, i] = g.transpose(1, 0, 2)
    mt = mt.reshape(DEPTH, 16, 64, 4, 2, 64).transpose(0, 1, 2, 4, 3, 5)
    return np.ascontiguousarray(mt)


def _rope_tables():
    t = np.arange(SEQ)
    row = np.concatenate([t // 64, np.zeros(NMETA, np.int64)]).astype(np.float32)
    col = np.concatenate([t % 64, np.zeros(NMETA, np.int64)]).astype(np.float32)
    inv = (np.float32(10000.0) ** (-np.arange(0, 32, 2, dtype=np.float32) / np.float32(32))).astype(np.float32)
    ar = row[:, None] * inv[None]
    ac = col[:, None] * inv[None]
    ang = np.concatenate([ar, ar, ac, ac], axis=-1).astype(np.float32)
    cos = np.cos(ang).astype(np.float32).T
    sin = np.sin(ang).astype(np.float32).T
    return np.ascontiguousarray(np.tile(cos, (2, 1))), np.ascontiguousarray(np.tile(sin, (2, 1)))


def _rot_mats():
    R = np.zeros((64, 64), np.float32)
    for d in range(16):
        R[d, d + 16] = -1.0
        R[d + 16, d] = 1.0
        R[32 + d, 48 + d] = -1.0
        R[48 + d, 32 + d] = 1.0
    lhsT = np.zeros((128, 128), np.float32)
    lhsT[:64, :64] = R.T
    lhsT[64:, 64:] = R.T
    bo = np.zeros((128, 128), np.float32)
    bo[:64, :64] = 1.0 / 64
    bo[64:, 64:] = 1.0 / 64
    return lhsT, bo


def _vpat():
    p = np.zeros((128, 194), np.float32)
    p[:, 64] = 1.0
    p[:, 66] = 1.0
    return p


_CACHE = {}


def kernel(x, meta_tokens, norm_g, w_in, q_norm_g, k_norm_g, na_rpb, w_o_attn, w_o_na, w_out, final_norm_g):
    x = np.asarray(x, np.float32)
    f = lambda a: np.ascontiguousarray(np.asarray(a, np.float32))
    if "nc" not in _CACHE:
        _CACHE["nc"] = build_nc()[0]
    nc = _CACHE["nc"]
    cosT, sinT = _rope_tables()
    rmat, bones = _rot_mats()
    shared = {
        "meta": f(meta_tokens),
        "ng": np.ascontiguousarray(np.broadcast_to(f(norm_g)[:, None, :], (DEPTH, 128, DM))),
        "w_in": f(w_in),
        "qg": np.ascontiguousarray(np.tile(f(q_norm_g), (1, 2))[:, :, None]),
        "kg": np.ascontiguousarray(np.tile(f(k_norm_g), (1, 2))[:, :, None]),
        "mtab": _host_consts(f(na_rpb)),
        "w_oa": f(w_o_attn), "w_ob": f(w_o_na), "w_out": f(w_out),
        "fg": np.ascontiguousarray(np.broadcast_to(f(final_norm_g)[None, :], (128, DM))),
        "cosT": cosT, "sinT": sinT, "rmat": rmat, "bones": bones, "eye": np.eye(128, dtype=np.float32),
        "vpat": _vpat(),
    }
    in_maps = []
    for b in range(NCORES):
        m = dict(shared)
        m["x"] = np.ascontiguousarray(x[b])
        in_maps.append(m)
    res = run_bass_kernel_spmd(nc, in_maps, core_ids=list(range(NCORES)))
    return np.stack([np.asarray(res.results[b]["out"], np.float32) for b in range(NCORES)], axis=0)
```

```python
import contextlib
import numpy as np
import concourse.bass as bass
import concourse.mybir as mybir
from concourse.bass_utils import run_bass_kernel_spmd

F32 = mybir.dt.float32
BF16 = mybir.dt.bfloat16
AF = mybir.ActivationFunctionType
ALU = mybir.AluOpType

NCORES = 8
DM = 1024
SEQ = 4096
NMETA = 16
LT = SEQ + NMETA
DEPTH = 2
INC = 5376
EPS = 1e-6
NEG = -30000.0
C_QA, C_KA, C_VA, C_ZA, C_QB, C_KB, C_VB, C_ZB, C_GA, C_GB = 0, 512, 640, 768, 1280, 1792, 2304, 2816, 3328, 4352
ENGS = ("pe", "act", "dve", "pool", "sp")
NRING = 6


class _Op:
    __slots__ = ("eng", "fn", "waits", "sig", "is_dma", "key", "idx", "ph")

    def __init__(self, eng, fn, is_dma, key):
        self.eng, self.fn, self.is_dma, self.key = eng, fn, is_dma, key
        self.waits = []
        self.sig = None


class Prog:
    def __init__(self):
        self.ops = []
        self.last_w = {}
        self.readers = {}
        self.deps = []
        self.phase = ""

    def _add(self, eng, fn, reads, writes, is_dma, key):
        o = _Op(eng, fn, is_dma, key)
        o.idx = len(self.ops)
        o.ph = self.phase
        self.ops.append(o)
        prods = set()
        for r in reads:
            w = self.last_w.get(r)
            if w is not None:
                prods.add(w)
        for r in writes:
            w = self.last_w.get(r)
            if w is not None:
                prods.add(w)
            prods.update(self.readers.get(r, ()))
        prods.discard(o)
        for p in prods:
            if (not p.is_dma) and (not is_dma) and p.eng == "pe" and eng == "pe":
                continue
            self.deps.append((p, o))
        for r in reads:
            self.readers.setdefault(r, []).append(o)
        for r in writes:
            self.last_w[r] = o
            self.readers[r] = []
        return o

    def op(self, eng, fn, reads=(), writes=()):
        return self._add(eng, fn, tuple(reads), tuple(writes), False, None)

    def dma(self, eng, fn, reads=(), writes=(), key=None):
        if key is None:
            key = writes[0]
        return self._add(eng, fn, tuple(reads), tuple(writes), True, ("dma", key))

    def emit(self, nc):
        prod_set = set(p for p, _ in self.deps)
        counters = {}
        hist = {}
        for o in self.ops:
            if o in prod_set or o.is_dma:
                sk = o.key if o.is_dma else ("eng", o.eng)
                counters[sk] = counters.get(sk, 0) + (16 if o.is_dma else 1)
                o.sig = (sk, counters[sk])
                if o.is_dma:
                    hist.setdefault(sk, []).append((o.idx, counters[sk]))
        by_cons = {}
        for p, c in self.deps:
            by_cons.setdefault(c, []).append(p)
        waited = {e: {} for e in ENGS}
        for o in self.ops:
            need = {}
            for p in by_cons.get(o, ()):
                sk, v = p.sig
                if p.is_dma:
                    for (ii, vv) in hist[sk]:
                        if ii < o.idx:
                            v = max(v, vv)
                        else:
                            break
                if v > need.get(sk, 0):
                    need[sk] = v
            for sk, v in need.items():
                if waited[o.eng].get(sk, 0) >= v:
                    continue
                waited[o.eng][sk] = v
                o.waits.append((sk, v))
        semkeys = list(counters.keys())
        with contextlib.ExitStack() as st:
            sems = {}
            for i, sk in enumerate(semkeys):
                sems[sk] = st.enter_context(nc.semaphore("s%d" % i))
            block = st.enter_context(nc.Block())
            per = {e: [o for o in self.ops if o.eng == e] for e in ENGS}

            def run(eng_obj, lst):
                for o in lst:
                    for sk, v in o.waits:
                        eng_obj.wait_ge(sems[sk], v)
                    ins = o.fn(eng_obj)
                    if o.sig is not None:
                        ins.then_inc(sems[o.sig[0]], 16 if o.is_dma else 1)

            @block.tensor
            def _(e):
                run(e, per["pe"])

            @block.scalar
            def _(e):
                run(e, per["act"])

            @block.vector
            def _(e):
                run(e, per["dve"])

            @block.gpsimd
            def _(e):
                run(e, per["pool"])

            @block.sync
            def _(e):
                run(e, per["sp"])
                for sk in semkeys:
                    if sk[0] == "dma":
                        e.wait_ge(sems[sk], counters[sk])


def _groups():
    gs = []
    for g in range(8):
        gs.append((512 * g, 512, [(512 * g + 128 * i, 128) for i in range(4)]))
    gs.append((SEQ, NMETA, [(SEQ, NMETA)]))
    return gs


def build_nc(depth=DEPTH, debug=False):
    nc = bass.Bass("TRN2", target_bir_lowering=False)

    def din(name, shape):
        return nc.dram_tensor(name, shape, F32, kind="ExternalInput").ap()

    x = din("x", [SEQ, DM])
    meta = din("meta", [NMETA, DM])
    ng = din("ng", [DEPTH, 128, DM])
    w_in = din("w_in", [DEPTH, DM, INC])
    qg = din("qg", [DEPTH, 128, 1])
    kg = din("kg", [DEPTH, 128, 1])
    mtab = din("mtab", [DEPTH, 16, 64, 2, 4, 64])
    w_oa = din("w_oa", [DEPTH, 512, DM])
    w_ob = din("w_ob", [DEPTH, 512, DM])
    w_out = din("w_out", [DEPTH, DM, DM])
    fg = din("fg", [128, DM])
    cosT = din("cosT", [128, LT])
    sinT = din("sinT", [128, LT])
    rmat = din("rmat", [128, 128])
    bones = din("bones", [128, 128])
    eye = din("eye", [128, 128])
    vpat = din("vpat", [128, 194])
    out = nc.dram_tensor("out", [SEQ, DM], F32, kind="ExternalOutput").ap()
    hbuf = nc.dram_tensor("hbuf", [LT, DM], F32).ap()
    hnts = nc.dram_tensor("hnts", [128, 8, LT], BF16).ap()
    kbs = nc.dram_tensor("kbs", [128, 4, LT], BF16).ap()
    vbs = nc.dram_tensor("vbs", [LT, 512], BF16).ap()
    dk = "ExternalOutput" if debug else "Internal"
    yas = nc.dram_tensor("yas", [128, 8, LT], F32, kind=dk).ap()
    ybs = nc.dram_tensor("ybs", [128, 8, LT], F32, kind=dk).ap()

    P = Prog()
    dbg_list = []

    def dbg(name, ap, reads, dt):
        if not debug:
            return
        shape = list(ap.shape)
        d = nc.dram_tensor("dbg_" + name, shape, dt, kind="ExternalOutput").ap()
        P.dma("sp", lambda e: e.dma_start(out=d, in_=ap), reads=reads, writes=[("dbg", name)], key=("dbg", len(dbg_list) % 4))
        dbg_list.append(name)

    with contextlib.ExitStack() as st:
        def sb(name, shape, dt):
            return st.enter_context(nc.sbuf_tensor(name, shape, dt))

        def ps(name, shape, dt):
            return st.enter_context(nc.psum_tensor(name, shape, dt))

        WIN = sb("WIN", [128, 8, 2048], BF16)
        W2 = sb("W2", [128, 8, 1024], BF16)
        KAT = sb("KAT", [128, LT], BF16)
        VA1 = sb("VA1", [128, 33, 194], BF16)
        BI = sb("BI", [128, 5, 2, 4, 128], BF16)
        BB = sb("BB", [128, 2, 2, 4, 128], BF16)
        KBR = sb("KBR", [128, 4, NRING * 128], BF16)
        VBR = sb("VBR", [128, NRING, 4, 194], BF16)
        KBM = sb("KBM", [128, 4, 16], BF16)
        VBM = sb("VBM", [16, 4, 194], BF16)
        HN = [sb("HN%d" % i, [128, DM], BF16) for i in range(2)]
        HNT = sb("HNT", [128, 8, 512], BF16)
        QAT = sb("QAT", [128, 4, 512], BF16)
        QBT = sb("QBT", [128, 4, 512], BF16)
        PTS = [sb("PTS%d" % i, [128, 2, 512], BF16) for i in range(3)]
        NF2 = 14
        COSB = sb("COSB", [128, 512], F32)
        SINB = sb("SINB", [128, 512], F32)
        F2 = [sb("F2_%d" % i, [128, 512], F32) for i in range(NF2)]
        NF4 = 4
        F4 = [sb("F4_%d" % i, [128, 2, 512], F32) for i in range(NF4)]
        OA = sb("OA", [128, 8, 512], BF16)
        OB = sb("OB", [128, 4, 512], BF16)
        GV = sb("GV", [128, DM], F32)
        IDB = sb("IDB", [128, 128], BF16)
        BON = sb("BON", [128, 128], F32)
        RMT = sb("RMT", [128, 128], F32)
        SEL = sb("SEL", [128, 64], F32)
        DMY = sb("DMY", [1, 8], F32)
        PATT = sb("PATT", [128, 194], BF16)
        IDF = sb("IDF", [128, 128], F32)
        RTS = [sb("RTS%d" % i, [128, 8], F32) for i in range(2)]
        QG = sb("QG", [128, 1], F32)
        KG = sb("KG", [128, 1], F32)
        ST = sb("ST", [128, 16], F32)
        PSB = [ps("PS%d" % i, [128, 2, 512], F32) for i in range(4)]
        PD = PSB[3][:, 0, :]
        PT = PSB[3][:, 1, :].bitcast(BF16).rearrange("p (a b) -> p a b", a=8)

        def bank(i):
            return PSB[i // 2][:, i % 2, :]

        ctr = {"f2": 0, "f4": 0, "hn": 0, "pts": 0, "st": 0, "bb": 0, "rts": 0}

        def rts():
            i = ctr["rts"] % 2
            ctr["rts"] += 1
            return RTS[i], ("RTS", i)


        def f2():
            i = ctr["f2"] % NF2
            ctr["f2"] += 1
            return F2[i], ("F2", i)

        def f4():
            i = ctr["f4"] % NF4
            ctr["f4"] += 1
            return F4[i], ("F4", i)

        def stcol():
            i = ctr["st"] % 16
            ctr["st"] += 1
            return ST[:, i:i + 1], ("ST", i)

        P.dma("pool", lambda e: e.dma_start(out=IDB[:], in_=eye), writes=["IDB"])
        P.dma("sp", lambda e: e.dma_start(out=BON[:], in_=bones), writes=["BON"])
        P.dma("sp", lambda e: e.dma_start(out=IDF[:], in_=eye), writes=["IDF"])
        P.dma("sp", lambda e: e.dma_start(out=RMT[:], in_=rmat), writes=["RMT"])
        P.op("pool", lambda e: e.memset(SEL[:], 0.0), writes=["SEL"])
        P.op("pool", lambda e: e.memset(SEL[64:65, :], 1.0), writes=["SEL"])
        P.dma("pool", lambda e: e.dma_start(out=PATT[:], in_=vpat), writes=["PATT"])
        P.op("dve", lambda e: e.tensor_copy(out=VA1[:], in_=PATT[:, :].unsqueeze(1).to_broadcast([128, 33, 194])), reads=["PATT"], writes=["VA1i"])
        P.op("dve", lambda e: e.tensor_copy(out=VBR[:].rearrange("p a b c -> p (a b) c"),
                                            in_=PATT[:, :].unsqueeze(1).to_broadcast([128, NRING * 4, 194])), reads=["PATT"], writes=["VBRi"])
        P.op("dve", lambda e: e.tensor_copy(out=VBM[:], in_=PATT[0:16, :].unsqueeze(1).to_broadcast([16, 4, 194])), reads=["PATT"], writes=["VBMi"])
        va_keys = [("VA1", t) for t in range(33)] + [("VA1b", t) for t in range(33)]
        ring_v = [("VBR", s, par) for s in range(NRING) for par in range(2)]
        for k in va_keys:
            P.last_w[k] = P.last_w["VA1i"]
        for k in ring_v:
            P.last_w[k] = P.last_w["VBRi"]
        P.last_w["VBM"] = P.last_w["VBMi"]

        groups = _groups()

        def wk(d0, ncols, k):
            return [("WIN", b, k) for b in range(d0 // 128, (d0 + ncols - 1) // 128 + 1)]

        def wka(d0, ncols):
            return [x for k in range(8) for x in wk(d0, ncols, k)]

        def load_w_cols(dst3, l, c0, ncols, d0, key, semname="w"):
            src = w_in[l].rearrange("(k p) c -> p k c", p=128)
            for k in range(8):
                P.dma("pool", lambda e, k=k: e.dma_start(out=dst3[:, k, d0:d0 + ncols], in_=src[:, k, c0:c0 + ncols]),
                      writes=wk(d0, ncols, k), key=(semname, k))

        def hsrc(l, t0, rows):
            if l == 0:
                if t0 >= SEQ:
                    return meta[0:rows, :], ()
                return x[t0:t0 + rows, :], ()
            return hbuf[t0:t0 + rows, :], [("hbuf", t0 // 128)]

        def normrope_gen(src, src_keys, gvec, gkey, dst, dst_keys, n, COS, SIN, cs_keys, bq=0, br=1):
            SQ, ksq = f2()
            P.op("act", lambda e: e.activation(out=SQ[:, 0:n], in_=src, func=AF.Square), reads=src_keys, writes=[ksq])
            yield
            pq = bank(bq)[:, 0:n]
            P.op("pe", lambda e: e.matmul(pq, lhsT=BON[:], rhs=SQ[:, 0:n], start=True, stop=True),
                 reads=[ksq, "BON"], writes=[("ps", bq)])
            yield
            RT, krt = f2()
            P.op("act", lambda e: e.activation(out=RT[:, 0:n], in_=pq, func=AF.Sqrt, bias=EPS, scale=1.0),
                 reads=[("ps", bq)], writes=[krt])
            yield
            RS, krs = f2()
            P.op("dve", lambda e: e.reciprocal(out=RS[:, 0:n], in_=RT[:, 0:n]), reads=[krt], writes=[krs])
            yield
            QN, kqn = f2()
            P.op("dve", lambda e: e.scalar_tensor_tensor(out=QN[:, 0:n], in0=src, scalar=gvec, in1=RS[:, 0:n],
                                                         op0=ALU.mult, op1=ALU.mult),
                 reads=list(src_keys) + [krs, gkey], writes=[kqn])
            yield
            pr = bank(br)[:, 0:n]
            P.op("pe", lambda e: e.matmul(pr, lhsT=RMT[:], rhs=QN[:, 0:n], start=True, stop=True),
                 reads=[kqn, "RMT"], writes=[("ps", br)])
            A, ka = f2()
            P.op("pool", lambda e: e.tensor_tensor(out=A[:, 0:n], in0=QN[:, 0:n], in1=COS[:, 0:n], op=ALU.mult),
                 reads=[kqn] + cs_keys, writes=[ka])
            yield
            B, kb = f2()
            P.op("dve", lambda e: e.tensor_tensor(out=B[:, 0:n], in0=pr, in1=SIN[:, 0:n], op=ALU.mult),
                 reads=[("ps", br)] + cs_keys, writes=[kb])
            yield
            P.op("pool", lambda e: e.tensor_tensor(out=dst, in0=A[:, 0:n], in1=B[:, 0:n], op=ALU.add),
                 reads=[ka, kb], writes=dst_keys)
            yield

        def run_gens(gens):
            gens = list(gens)
            while gens:
                for g_ in list(gens):
                    try:
                        next(g_)
                    except StopIteration:
                        gens.remove(g_)

        def normrope(*a, **k):
            run_gens([normrope_gen(*a, **k)])

        def load_cs(tok0, n):
            COS, kc, SIN, ks = COSB, "COSB", SINB, "SINB"
            P.dma("sp", lambda e: e.dma_start(out=COS[:, 0:n], in_=cosT[:, tok0:tok0 + n]), writes=[kc])
            P.dma("sp", lambda e: e.dma_start(out=SIN[:, 0:n], in_=sinT[:, tok0:tok0 + n]), writes=[ks])
            return COS, SIN, [kc, ks]

        def rms_tile(l, t0, rows, gkey="GV"):
            HT, kht = f4()
            HTf = HT[:].rearrange("p a b -> p (a b)")
            src, skeys = hsrc(l, t0, rows)
            P.dma("sp", lambda e: e.dma_start(out=HTf[0:rows, :], in_=src), reads=skeys, writes=[kht])
            i = ctr["hn"] % 2
            ctr["hn"] += 1
            HNs, khn = HN[i], ("HN", i)
            ssq, kss = stcol()
            P.op("pool", lambda e: e.memset(ssq[0:rows, :], 0.0), writes=[kss])
            P.op("act", lambda e: e.activation(out=HNs[0:rows, :], in_=HTf[0:rows, :], func=AF.Square,
                                               accum_out=ssq[0:rows, :]), reads=[kht], writes=[khn, kss])
            rt, krt = stcol()
            P.op("act", lambda e: e.activation(out=rt[0:rows, :], in_=ssq[0:rows, :], func=AF.Sqrt, bias=EPS,
                                               scale=1.0 / DM), reads=[kss], writes=[krt])
            rs, krs = stcol()
            P.op("dve", lambda e: e.reciprocal(out=rs[0:rows, :], in_=rt[0:rows, :]), reads=[krt], writes=[krs])
            P.op("dve", lambda e: e.scalar_tensor_tensor(out=HNs[0:rows, :], in0=HTf[0:rows, :], scalar=rs[0:rows, :],
                                                         in1=GV[0:rows, :], op0=ALU.mult, op1=ALU.mult),
                 reads=[kht, krs, gkey], writes=[khn])
            return HTf, kht, HNs, khn

        for l in range(depth):
            last = (l == depth - 1)
            P.phase = "A%d" % l
            P.dma("sp", lambda e, l=l: e.dma_start(out=GV[:], in_=ng[l]), writes=["GV"])
            P.dma("sp", lambda e, l=l: e.dma_start(out=KG[:], in_=kg[l]), writes=["KG"])
            P.dma("sp", lambda e, l=l: e.dma_start(out=QG[:], in_=qg[l]), writes=["QG"])
            load_w_cols(WIN, l, C_KA, 256, 768, "WIN")
            load_w_cols(WIN, l, C_KB, 1024, 1024, "WIN")
            for c in range(4):
                P.dma("pool", lambda e, c=c, l=l: e.dma_start(out=W2[0:64, c, :], in_=w_oa[l, c * 64:(c + 1) * 64, :]), writes=[("W2", c)], key=("w2", c))
                P.dma("pool", lambda e, c=c, l=l: e.dma_start(out=W2[64:128, c, :], in_=w_oa[l, (4 + c) * 64:(5 + c) * 64, :]), writes=[("W2", c)], key=("w2", c))
                P.dma("pool", lambda e, c=c, l=l: e.dma_start(out=W2[:, 4 + c, :], in_=w_ob[l, c * 128:(c + 1) * 128, :]), writes=[("W2", 4 + c)], key=("w2", 4 + c))
            for c in range(4):
                load_w_cols(WIN, l, C_QA + 64 * c, 64, c * 128, "WIN", semname="wp")
                load_w_cols(WIN, l, C_QA + 256 + 64 * c, 64, c * 128 + 64, "WIN", semname="wp")
            for c in range(2):
                load_w_cols(WIN, l, C_ZA + 64 * c, 64, 512 + c * 128, "WIN", semname="wp")
                load_w_cols(WIN, l, C_ZA + 256 + 64 * c, 64, 512 + c * 128 + 64, "WIN", semname="wp")
            HB = [HNT, OA]

            def hk(bf, i):
                return ("HNT", i) if bf == 0 else ("OAH", i)

            oa_alias = [("OA", x_) for x_ in range(8)] + [("OAH", i_) for i_ in range(4)]
            P.op("pool", lambda e: e.memset(DMY[0:1, 0:1], 0.0), writes=oa_alias)

            def stage1_gen(grp, bf):
                tok0, n, tiles = grp
                HBb = HB[bf]

                def evac(i, rows):
                    if i % 2 == 0:
                        P.op("act", lambda e: e.copy(out=HBb[:, :, i * 128:i * 128 + rows], in_=PT[:, :, 0:rows]),
                             reads=[("ps", 7)], writes=[hk(bf, i)])
                    else:
                        P.op("dve", lambda e: e.tensor_copy(out=HBb[:, :, i * 128:i * 128 + rows], in_=PT[:, :, 0:rows]),
                             reads=[("ps", 7)], writes=[hk(bf, i)])

                prev_t = None
                for i, (t0, rows) in enumerate(tiles):
                    HTf, kht, HNs, khn = rms_tile(l, t0, rows)
                    yield
                    if prev_t is not None:
                        evac(*prev_t)
                    for k in range(8):
                        P.op("pe", lambda e, k=k, HNs=HNs, rows=rows: e.transpose(PT[:, k, 0:rows], HNs[0:rows, k * 128:(k + 1) * 128],
                                                                                  IDB[0:rows, 0:rows]),
                             reads=[khn, "IDB"], writes=[("ps", 7)])
                    prev_t = (i, rows)
                    yield
                evac(*prev_t)
                yield

            def proj_gen(grp, bf):
                tok0, n, tiles = grp
                HBb = HB[bf]
                hkeys = [hk(bf, i) for i in range(len(tiles))]
                tkeys = [t0 // 128 for (t0, _) in tiles]
                COS, SIN, cskeys = load_cs(tok0, n)
                P.dma("pool", lambda e: e.dma_start(out=hnts[:, :, tok0:tok0 + n], in_=HBb[:, :, 0:n]),
                      reads=hkeys, writes=[("hnts", t) for t in tkeys], key=("HNTst", bf))
                pk = bank(6)[:, 0:n]
                for k in range(8):
                    P.op("pe", lambda e, k=k: e.matmul(pk, lhsT=WIN[:, k, 768:896], rhs=HBb[:, k, 0:n], start=(k == 0), stop=(k == 7)),
                         reads=hkeys + wk(768, 128, k), writes=[("ps", 6)])
                yield
                nr = normrope_gen(pk, [("ps", 6)], KG[:, 0:1], "KG", KAT[:, tok0:tok0 + n], [("KAT", t) for t in tkeys], n, COS, SIN, cskeys)
                for i, (t0, rows) in enumerate(tiles):
                    pv = bank(2)[0:rows, 0:128]
                    for k in range(8):
                        P.op("pe", lambda e, k=k, i=i, rows=rows, pv=pv: e.matmul(pv, lhsT=HBb[:, k, i * 128:i * 128 + rows], rhs=WIN[:, k, 896:1024],
                                                                               start=(k == 0), stop=(k == 7)),
                             reads=[hk(bf, i)] + wk(896, 128, k), writes=[("ps", 2)])
                    tt = t0 // 128
                    P.op("act", lambda e, tt=tt, rows=rows, pv=pv: e.copy(out=VA1[0:rows, tt, 0:64], in_=pv[:, 0:64]),
                         reads=[("ps", 2)], writes=[("VA1", tt), "ps2rd"])
                    P.op("dve", lambda e, tt=tt, rows=rows, pv=pv: e.tensor_copy(out=VA1[0:rows, tt, 130:194], in_=pv[:, 64:128]),
                         reads=[("ps", 2)], writes=[("VA1b", tt), "ps2rd"])
                    nr = step_gen(nr)
                    yield
                for c in range(4):
                    bk = 3 + (c % 2)
                    pb = bank(bk)[:, 0:n]
                    for k in range(8):
                        P.op("pe", lambda e, k=k, c=c, pb=pb: e.matmul(pb, lhsT=WIN[:, k, 1024 + c * 128:1024 + (c + 1) * 128], rhs=HBb[:, k, 0:n],
                                                                      start=(k == 0), stop=(k == 7)),
                             reads=hkeys + wk(1024 + c * 128, 128, k), writes=[("ps", bk)])
                    P.op("dve", lambda e, c=c, pb=pb: e.tensor_copy(out=QAT[:, c, 0:n], in_=pb), reads=[("ps", bk)], writes=[("QAT", c)])
                    nr = step_gen(nr)
                    yield
                P.dma("pool", lambda e: e.dma_start(out=kbs[:, :, tok0:tok0 + n], in_=QAT[:, :, 0:n]),
                      reads=[("QAT", c) for c in range(4)], writes=[("kbs", t) for t in tkeys], key=("KBst",))
                for i, (t0, rows) in enumerate(tiles):
                    bk = 5 if i % 2 == 0 else 2
                    pv = bank(bk)[0:rows, :]
                    for k in range(8):
                        P.op("pe", lambda e, k=k, i=i, rows=rows, pv=pv: e.matmul(pv, lhsT=HBb[:, k, i * 128:i * 128 + rows], rhs=WIN[:, k, 1536:2048],
                                                                               start=(k == 0), stop=(k == 7)),
                             reads=[hk(bf, i)] + wk(1536, 512, k), writes=[("ps", bk)])
                    P.op("act", lambda e, i=i, rows=rows, pv=pv: e.copy(out=QBT[0:rows, i, :], in_=pv), reads=[("ps", bk)], writes=[("QBT", i)])
                    P.dma("pool", lambda e, i=i, rows=rows, t0=t0: e.dma_start(out=vbs[t0:t0 + rows, :], in_=QBT[0:rows, i, :]),
                          reads=[("QBT", i)], writes=[("vbs", t0 // 128)], key=("VBst", i))
                    nr = step_gen(nr)
                    yield
                while nr is not None:
                    nr = step_gen(nr)
                    yield

            def step_gen(g_):
                if g_ is None:
                    return None
                try:
                    next(g_)
                    return g_
                except StopIteration:
                    return None

            s1 = stage1_gen(groups[0], 0)
            while s1 is not None:
                s1 = step_gen(s1)
            for gi_, grp in enumerate(groups):
                bf = gi_ % 2
                pg = proj_gen(grp, bf)
                s1 = stage1_gen(groups[gi_ + 1], 1 - bf) if gi_ + 1 < len(groups) else None
                while s1 is not None or pg is not None:
                    s1 = step_gen(s1)
                    for _ in range(3):
                        pg = step_gen(pg)
            P.op("pool", lambda e: e.memset(DMY[0:1, 0:1], 0.0), writes=oa_alias)

            if last:
                b_groups = groups[:8]
            else:
                b_groups = groups
            P.phase = "B1w%d" % l
            load_w_cols(WIN, l, C_QB, 512, 1024, "WIN")
            for c in range(2, 4):
                load_w_cols(WIN, l, C_ZA + 64 * c, 64, 512 + c * 128, "WIN")
                load_w_cols(WIN, l, C_ZA + 256 + 64 * c, 64, 512 + c * 128 + 64, "WIN")
            load_w_cols(WIN, l, C_ZB, 512, 1536, "WIN")

            def build_bias(dst, dkey, t, cidx, l=l):
                for rk_l in range(2):
                    for rq_l in range(2):
                        rk = 2 * cidx + rk_l
                        rq = 2 * t + rq_l
                        rs = min(max(rq - 4, 0), 56)
                        idx = (rk - rq + 7) if (rs <= rk <= rs + 7) else 15
                        P.dma("pool", lambda e, rk_l=rk_l, rq_l=rq_l, idx=idx: e.dma_start(
                            out=dst[rk_l * 64:(rk_l + 1) * 64, :, :, rq_l * 64:(rq_l + 1) * 64], in_=mtab[l, idx]),
                            writes=[dkey], key=(dkey, rk_l, rq_l) if dkey[0] == "BB" else ("bias", rk_l, rq_l))

            for j in range(5):
                build_bias(BI[:, j], ("BI", j), 10, 10 + j - 2)
            P.dma("sp", lambda e: e.dma_start(out=KBM[:], in_=kbs[:, :, SEQ:LT]), reads=[("kbs", 32)], writes=["KBM"])
            for par in range(2):
                P.dma("sp", lambda e, par=par: e.dma_start(out=VBM[:, :, 130 * par:130 * par + 64],
                                                            in_=vbs[SEQ:LT, :].rearrange("p (a t d) -> p a t d", a=4, t=2)[:, :, par, :]),
                      reads=[("vbs", 32)], writes=["VBM"], key=("VBM", par))

            def ring_load(cidx):
                s = cidx % NRING
                P.dma("sp", lambda e: e.dma_start(out=KBR[:, :, s * 128:(s + 1) * 128], in_=kbs[:, :, cidx * 128:(cidx + 1) * 128]),
                      reads=[("kbs", cidx)], writes=[("KBR", s)])
                for par in range(2):
                    P.dma("sp", lambda e, par=par: e.dma_start(out=VBR[:, s, :, 130 * par:130 * par + 64],
                                                                in_=vbs[cidx * 128:(cidx + 1) * 128, :].rearrange("p (a t d) -> p a t d", a=4, t=2)[:, :, par, :]),
                          reads=[("vbs", cidx)], writes=[("VBR", s, par)], key=("VBR", s, par))

            ring_next = 0

            def pair_finish(zmm, zper, ncols, blocks, out_fn, out_keys):
                P4, P5, P6 = bank(4), bank(5), bank(6)
                X4, kx4 = f2()
                P.op("act", lambda e: e.copy(out=X4[0:65, 0:ncols], in_=P4[0:65, 0:ncols]), reads=[("ps", 4)], writes=[kx4])
                X5, kx5 = f2()
                P.op("dve", lambda e: e.tensor_copy(out=X5[:, 0:ncols], in_=P5[:, 0:ncols]), reads=[("ps", 5)], writes=[kx5])
                yield
                for zi, zf in enumerate(zmm):
                    zf()
                    if zi % zper == zper - 1:
                        yield
                TH, kth = f2()
                P.op("act", lambda e: e.activation(out=TH[:, 0:ncols], in_=P6[:, 0:ncols], func=AF.Tanh, scale=0.5), reads=[("ps", 6)], writes=[kth])
                ZS, kzs = f2()
                P.op("dve", lambda e: e.scalar_tensor_tensor(out=ZS[:, 0:ncols], in0=TH[:, 0:ncols], scalar=1.0, in1=P6[:, 0:ncols],
                                                             op0=ALU.add, op1=ALU.mult), reads=[kth, ("ps", 6)], writes=[kzs])
                T1, kt1 = f2()
                P.op("dve", lambda e: e.scalar_tensor_tensor(out=T1[0:64, 0:ncols], in0=X4[0:64, 0:ncols], scalar=0.5, in1=ZS[0:64, 0:ncols],
                                                             op0=ALU.mult, op1=ALU.mult), reads=[kx4, kzs], writes=[kt1])
                P.op("dve", lambda e: e.scalar_tensor_tensor(out=T1[64:128, 0:ncols], in0=X5[64:128, 0:ncols], scalar=0.5, in1=ZS[64:128, 0:ncols],
                                                             op0=ALU.mult, op1=ALU.mult), reads=[kx5, kzs, kt1], writes=[kt1])
                yield
                NB = len(blocks)
                for bi_, (c0, w) in enumerate(blocks):
                    P.op("pe", lambda e, bi_=bi_, c0=c0, w=w: e.matmul(P6[0:w, bi_:bi_ + 1], lhsT=X4[0:65, c0:c0 + w], rhs=SEL[0:65, 0:1],
                                                                      start=True, stop=True), reads=[kx4, "SEL"], writes=[("ps", 6)])
                    P.op("pe", lambda e, bi_=bi_, c0=c0, w=w: e.matmul(P6[0:w, NB + bi_:NB + bi_ + 1], lhsT=X5[0:65, c0:c0 + w], rhs=IDF[0:65, 0:1],
                                                                      start=False, stop=True, skip_group_check=True), reads=[kx5, "IDF"], writes=[("ps", 6)])
                yield
                RT, krt = rts()
                P.op("dve", lambda e: e.reciprocal(out=RT[:, 0:2 * NB], in_=P6[:, 0:2 * NB]), reads=[("ps", 6)], writes=[krt])
                RTB, krb = f2()
                RTBv = RTB[:, :].rearrange("p (a b) -> p a b", a=4)
                P.op("dve", lambda e: e.tensor_copy(out=RTBv[:, 0:NB, 0:64], in_=RT[:, 0:NB].unsqueeze(2).to_broadcast([128, NB, 64])),
                     reads=[krt], writes=[krb])
                P.op("dve", lambda e: e.tensor_copy(out=RTBv[:, 0:NB, 64:128], in_=RT[:, NB:2 * NB].unsqueeze(2).to_broadcast([128, NB, 64])),
                     reads=[krt, krb], writes=[krb])
                for bi_, (c0, w) in enumerate(blocks):
                    P.op("pe", lambda e, bi_=bi_, c0=c0, w=w: e.matmul(P6[:, c0:c0 + w], lhsT=RTBv[0:w, bi_, :], rhs=IDF[0:w, 0:w],
                                                                      start=(bi_ == 0), stop=True, skip_group_check=True),
                         reads=[krb, "IDF"], writes=[("ps", 6)])
                yield
                P.op("dve", lambda e: out_fn(e, T1, P6), reads=[kt1, ("ps", 6)], writes=out_keys)
                yield

            def step_gen(g_):
                if g_ is None:
                    return None
                try:
                    next(g_)
                    return g_
                except StopIteration:
                    return None

            def drain_gen(g_):
                while g_ is not None:
                    g_ = step_gen(g_)

            def bg_outproj(which, tok0p, np_, tkp, rkeys_fn):
                SRC = OA if which == 0 else OB
                dst = yas if which == 0 else ybs
                dname = "yas" if which == 0 else "ybs"
                for dc in range(8):
                    py = bank(7)[:, 0:np_]
                    for c in range(4):
                        P.op("pe", lambda e, c=c, dc=dc: e.matmul(py, lhsT=W2[:, 4 * which + c, dc * 128:(dc + 1) * 128], rhs=SRC[:, c, 0:np_],
                                                                 start=(c == 0), stop=(c == 3)),
                             reads=rkeys_fn(c) + [("W2", 4 * which + c)], writes=[("ps", 7)])
                        if c % 2 == 1:
                            yield
                    Y, ky = f2()
                    P.op("dve", lambda e, Y=Y: e.tensor_copy(out=Y[:, 0:np_], in_=py), reads=[("ps", 7)], writes=[ky])
                    P.dma("pool", lambda e, dc=dc, Y=Y: e.dma_start(out=dst[:, dc, tok0p:tok0p + np_], in_=Y[:, 0:np_]),
                          reads=[ky], writes=[(dname, dc, t) for t in tkp], key=ky)
                    yield
                    yield

            def bg_qB(n_, hk):
                for c in range(4):
                    pk = bank(7)[:, 0:n_]
                    for k in range(8):
                        P.op("pe", lambda e, k=k, c=c: e.matmul(pk, lhsT=WIN[:, k, 1024 + c * 128:1024 + (c + 1) * 128], rhs=HNT[:, k, 0:n_],
                                                               start=(k == 0), stop=(k == 7)),
                             reads=hk + wk(1024 + c * 128, 128, k), writes=[("ps", 7)])
                        if k % 2 == 1:
                            yield
                    P.op("dve", lambda e, c=c: e.tensor_scalar(out=QBT[:, c, 0:n_], in0=pk, scalar1=0.125, scalar2=None, op0=ALU.mult),
                         reads=[("ps", 7)], writes=[("QBT", c)])
                    yield
                    yield

            def chain_gens(*gs):
                for g_ in gs:
                    if g_ is not None:
                        yield from g_

            prev_bg = []

            for (tok0, n, tiles) in b_groups:
                P.dma("sp", lambda e, tok0=tok0, n=n: e.dma_start(out=HNT[:, :, 0:n], in_=hnts[:, :, tok0:tok0 + n]),
                      reads=[("hnts", t0 // 128) for (t0, _) in tiles], writes=[("HNT", i) for i in range(4)], key=("HNTld",))
                hkeys = [("HNT", i) for i in range(4)]
                COS, SIN, cskeys = load_cs(tok0, n)
                P.phase = "qA%d" % l
                for c0 in (0, 2):
                    gens = []
                    for ci_, c in enumerate((c0, c0 + 1)):
                        bk = 6 + ci_
                        pk = bank(bk)[:, 0:n]
                        for k in range(8):
                            P.op("pe", lambda e, k=k, c=c, pk=pk, n=n: e.matmul(pk, lhsT=WIN[:, k, c * 128:(c + 1) * 128], rhs=HNT[:, k, 0:n],
                                                                       start=(k == 0), stop=(k == 7)),
                                 reads=hkeys + wk(c * 128, 128, k), writes=[("ps", bk)])
                        gens.append(normrope_gen(pk, [("ps", bk)], QG[:, 0:1], "QG", QAT[:, c, 0:n], [("QAT", c)], n, COS, SIN, cskeys,
                                                 bq=2 * ci_, br=2 * ci_ + 1))
                    run_gens(gens)
                P.phase = "GQA%d" % l
                P.phase = "GQA%d" % l
                PC = PSB[2]
                pending = []

                def gqa_S(c, kt, n=n):
                    k0 = kt * 128
                    kn = 128 if kt < 32 else NMETA
                    PSs = PSB[kt % 2]
                    skeys = [("ps", 2 * (kt % 2)), ("ps", 2 * (kt % 2) + 1)]
                    for j in range(2):
                        P.op("pe", lambda e, j=j: e.matmul(
                            PSs[0:kn, j, 0:n], lhsT=KAT[64 * j:64 * j + 64, k0:k0 + kn], rhs=QAT[64 * j:64 * j + 64, c, 0:n], start=True, stop=True),
                            reads=[("KAT", kt), ("QAT", c)], writes=[skeys[j]])

                def gqa_E(c, kt, n=n):
                    kn = 128 if kt < 32 else NMETA
                    PSs = PSB[kt % 2]
                    skeys = [("ps", 2 * (kt % 2)), ("ps", 2 * (kt % 2) + 1)]
                    pi = ctr["pts"] % 3
                    ctr["pts"] += 1
                    PTs, kpt = PTS[pi], ("PTS", pi)
                    P.op("act", lambda e: e.activation(out=PTs[0:kn, :, 0:n], in_=PSs[0:kn, :, 0:n], func=AF.Exp, scale=0.125),
                         reads=skeys, writes=[kpt])
                    return PTs, kpt

                def gqa_PV(c, kt, PTs, kpt, n=n):
                    kn = 128 if kt < 32 else NMETA
                    P.op("pe", lambda e: e.matmul(bank(4)[0:65, 0:n], lhsT=VA1[0:kn, kt, 0:65], rhs=PTs[0:kn, 0, 0:n], start=(kt == 0), stop=(kt == 32)),
                         reads=[("VA1", kt), kpt], writes=[("ps", 4)])
                    P.op("pe", lambda e: e.matmul(bank(5)[:, 0:n], lhsT=VA1[0:kn, kt, 66:194], rhs=PTs[0:kn, 1, 0:n], start=(kt == 0), stop=(kt == 32)),
                         reads=[("VA1b", kt), kpt], writes=[("ps", 5)])

                def gqa_finish(c, n=n):
                    P6 = bank(6)
                    zmm = []
                    for k in range(8):
                        zmm.append(lambda k=k: P.op("pe", lambda e: e.matmul(P6[:, 0:n], lhsT=WIN[:, k, 512 + c * 128:512 + (c + 1) * 128], rhs=HNT[:, k, 0:n],
                                                                            start=(k == 0), stop=(k == 7)),
                                                    reads=hkeys + wk(512 + c * 128, 128, k), writes=[("ps", 6)]))
                    nb = (n + 127) // 128
                    blocks = [(qb * 128, min(128, n - qb * 128)) for qb in range(nb)]
                    return pair_finish(zmm, 2, n, blocks,
                                       lambda e, T1, BC: e.tensor_tensor(out=OA[:, c, 0:n], in0=T1[:, 0:n], in1=BC[:, 0:n], op=ALU.mult),
                                       [("OA", c)])

                bg_rest = chain_gens(prev_bg[1] if prev_bg else None, bg_qB(n, hkeys))
                prev_bg = []
                fin = None
                for c in range(4):
                    gqa_S(c, 0)
                    gqa_S(c, 1)
                    fin = step_gen(fin)
                    for kt in range(33):
                        PTs_, kpt_ = gqa_E(c, kt)
                        if kt + 2 < 33:
                            gqa_S(c, kt + 2)
                        gqa_PV(c, kt, PTs_, kpt_)
                        fin = step_gen(fin)
                        bg_rest = step_gen(bg_rest)
                    drain_gen(fin)
                    fin = gqa_finish(c)
                drain_gen(fin)
                drain_gen(bg_rest)
                if (tok0, n) == (b_groups[-1][0], b_groups[-1][1]):
                    load_w_cols(WIN, l, C_GA, 512, 0, "WIN", semname="wp")
                    load_w_cols(WIN, l, C_GB, 512, 1024, "WIN", semname="wp")
                    load_w_cols(WIN, l, C_GA + 512, 512, 512, "WIN", semname="wp")
                tkeys = [t0 // 128 for (t0, _) in tiles]
                P.phase = "NA%d" % l
                fin = None
                bg_na = bg_outproj(0, tok0, n, tkeys, lambda c_: [("OA", c_)])
                for i, (t0, rows) in enumerate(tiles):
                    is_meta = (t0 >= SEQ)
                    t = t0 // 128
                    chunks = []
                    interior = False
                    if not is_meta:
                        r0, r1 = 2 * t, 2 * t + 1
                        lo = min(max(r0 - 4, 0), 56) // 2
                        hi = (min(max(r1 - 4, 0), 56) + 7) // 2
                        want = min(max(t + 3, hi), 31)
                        while ring_next <= want:
                            ring_load(ring_next)
                            ring_next += 1
                        interior = (2 <= t <= 29)
                        chunks = list(range(lo, hi + 1))
                    nch = len(chunks) + 1
                    q0 = i * 128

                    def na_S(ci, chunks=chunks, t=t, rows=rows, q0=q0, interior=interior):
                        PSs = PSB[ci % 2]
                        skeys = [("ps", 2 * (ci % 2)), ("ps", 2 * (ci % 2) + 1)]
                        if ci < len(chunks):
                            cidx = chunks[ci]
                            s_ = cidx % NRING
                            if interior:
                                BT, kbt = BI[:, cidx - t + 2], ("BI", cidx - t + 2)
                            else:
                                bi = ctr["bb"] % 2
                                ctr["bb"] += 1
                                BT, kbt = BB[:, bi], ("BB", bi)
                                build_bias(BT, kbt, t, cidx)
                            for par in range(2):
                                P.op("pe", lambda e, par=par: e.matmul(PSs[:, par, :], lhsT=IDB[:], rhs=BT[:, par].rearrange("p a b -> p (a b)"),
                                                                       start=True, stop=False, skip_group_check=True),
                                     reads=[kbt, "IDB"], writes=[skeys[par]])
                            for p4 in range(4):
                                for par in range(2):
                                    P.op("pe", lambda e, par=par, p4=p4: e.matmul(
                                        PSs[:, par, p4 * 128:p4 * 128 + rows], lhsT=KBR[64 * par:64 * par + 64, p4, s_ * 128:(s_ + 1) * 128],
                                        rhs=QBT[64 * par:64 * par + 64, p4, q0:q0 + rows], start=False, stop=True, skip_group_check=True),
                                        reads=[("KBR", s_), ("QBT", p4)], writes=[skeys[par]])
                        else:
                            for p4 in range(4):
                                for par in range(2):
                                    P.op("pe", lambda e, par=par, p4=p4: e.matmul(
                                        PSs[0:16, par, p4 * 128:p4 * 128 + rows], lhsT=KBM[64 * par:64 * par + 64, p4, :],
                                        rhs=QBT[64 * par:64 * par + 64, p4, q0:q0 + rows], start=(p4 == 0), stop=True, skip_group_check=True),
                                        reads=["KBM", ("QBT", p4)], writes=[skeys[par]])

                    def na_E(ci, chunks=chunks, rows=rows):
                        PSs = PSB[ci % 2]
                        skeys = [("ps", 2 * (ci % 2)), ("ps", 2 * (ci % 2) + 1)]
                        pi = ctr["pts"] % 3
                        ctr["pts"] += 1
                        PTs, kpt = PTS[pi], ("PTS", pi)
                        if ci != len(chunks):
                            P.op("act", lambda e: e.activation(out=PTs[:, :, :], in_=PSs[:, :, :], func=AF.Exp), reads=skeys, writes=[kpt])
                        else:
                            P.op("act", lambda e: e.activation(
                                out=PTs[0:16].rearrange("p a (b c) -> p a b c", b=4)[:, :, :, 0:rows],
                                in_=PSs[0:16].rearrange("p a (b c) -> p a b c", b=4)[:, :, :, 0:rows], func=AF.Exp),
                                reads=skeys, writes=[kpt])
                        return PTs, kpt

                    def na_PV(ci, PTs, kpt, chunks=chunks, rows=rows):
                        last_c = (ci == len(chunks))
                        s_ = (chunks[ci] % NRING) if not last_c else 0
                        for p4 in range(4):
                            for par in range(2):
                                if par == 0:
                                    outp = bank(4)[0:65, p4 * 128:p4 * 128 + rows]
                                    lo, hi = 0, 65
                                else:
                                    outp = bank(5)[:, p4 * 128:p4 * 128 + rows]
                                    lo, hi = 66, 194
                                if not last_c:
                                    P.op("pe", lambda e, par=par, p4=p4, outp=outp, lo=lo, hi=hi: e.matmul(
                                        outp, lhsT=VBR[:, s_, p4, lo:hi], rhs=PTs[:, par, p4 * 128:p4 * 128 + rows],
                                        start=(ci == 0 and p4 == 0), stop=False, skip_group_check=True),
                                        reads=[("VBR", s_, par), kpt], writes=[("ps", 4 + par)])
                                else:
                                    P.op("pe", lambda e, par=par, p4=p4, outp=outp, lo=lo, hi=hi: e.matmul(
                                        outp, lhsT=VBM[0:16, p4, lo:hi], rhs=PTs[0:16, par, p4 * 128:p4 * 128 + rows],
                                        start=(ci == 0 and p4 == 0), stop=True, skip_group_check=True),
                                        reads=["VBM", kpt], writes=[("ps", 4 + par)])

                    def na_finish(i=i, rows=rows, q0=q0):
                        P6 = bank(6)
                        zmm = []
                        for p4 in range(4):
                            for k in range(8):
                                zmm.append(lambda p4=p4, k=k: P.op("pe", lambda e: e.matmul(
                                    P6[:, p4 * 128:p4 * 128 + rows], lhsT=WIN[:, k, 1536 + p4 * 128:1536 + (p4 + 1) * 128],
                                    rhs=HNT[:, k, q0:q0 + rows], start=(k == 0), stop=(k == 7), skip_group_check=True),
                                    reads=[("HNT", i)] + wk(1536 + p4 * 128, 128, k), writes=[("ps", 6)]))
                        blocks = [(p4 * 128, rows) for p4 in range(4)]
                        return pair_finish(zmm, 32, 512, blocks,
                                           lambda e, T1, BC: e.tensor_tensor(
                                               out=OB[:, :, q0:q0 + rows], in0=T1[:, :].rearrange("p (a b) -> p a b", a=4)[:, :, 0:rows],
                                               in1=BC[:, :].rearrange("p (a b) -> p a b", a=4)[:, :, 0:rows], op=ALU.mult),
                                           [("OB", i)])

                    na_S(0)
                    if nch > 1:
                        na_S(1)
                    fin = step_gen(fin)
                    for ci in range(nch):
                        PTs_, kpt_ = na_E(ci)
                        if ci + 2 < nch:
                            na_S(ci + 2)
                        na_PV(ci, PTs_, kpt_)
                        fin = step_gen(fin)
                        bg_na = step_gen(bg_na)
                        bg_na = step_gen(bg_na)
                    drain_gen(fin)
                    fin = na_finish()
                drain_gen(fin)
                drain_gen(bg_na)
                obk = [("OB", i_) for i_ in range(len(tiles))]
                prev_bg = [None, bg_outproj(1, tok0, n, tkeys, lambda c_, obk=obk: list(obk))]
            P.phase = "yflush%d" % l
            for g_ in prev_bg:
                drain_gen(g_)
            prev_bg = []

            P.phase = "B2%d" % l
            load_w_cols(WIN, l, C_GB + 512, 512, 1536, "WIN")
            for k in range(8):
                P.dma("pool", lambda e, k=k, l=l: e.dma_start(out=W2[:, k, :], in_=w_out[l, k * 128:(k + 1) * 128, :]), writes=[("W2", k)], key=("w2", k))
            if last:
                P.dma("sp", lambda e: e.dma_start(out=GV[:], in_=fg), writes=["GV"])
            MIX = OA
            for (tok0, n, tiles) in b_groups:
                tkeys = [t0 // 128 for (t0, _) in tiles]
                P.dma("sp", lambda e, tok0=tok0, n=n: e.dma_start(out=HNT[:, :, 0:n], in_=hnts[:, :, tok0:tok0 + n]),
                      reads=[("hnts", t) for t in tkeys], writes=[("HNT", i) for i in range(4)], key=("HNTld",))
                hkeys = [("HNT", i) for i in range(4)]
                for dc in range(8):
                    YA, kya = f2()
                    YB, kyb = f2()
                    P.dma("sp", lambda e, dc=dc, tok0=tok0, n=n, YA=YA: e.dma_start(out=YA[:, 0:n], in_=yas[:, dc, tok0:tok0 + n]),
                          reads=[("yas", dc, t) for t in tkeys], writes=[kya])
                    P.dma("sp", lambda e, dc=dc, tok0=tok0, n=n, YB=YB: e.dma_start(out=YB[:, 0:n], in_=ybs[:, dc, tok0:tok0 + n]),
                          reads=[("ybs", dc, t) for t in tkeys], writes=[kyb])
                    ba, bb_ = (0, 1) if dc % 2 == 0 else (2, 3)
                    pa = bank(ba)[:, 0:n]
                    pb = bank(bb_)[:, 0:n]
                    for k in range(8):
                        P.op("pe", lambda e, k=k, dc=dc, n=n, pa=pa: e.matmul(pa, lhsT=WIN[:, k, dc * 128:(dc + 1) * 128], rhs=HNT[:, k, 0:n],
                                                                           start=(k == 0), stop=(k == 7)),
                             reads=hkeys + wk(dc * 128, 128, k), writes=[("ps", ba)])
                    for k in range(8):
                        P.op("pe", lambda e, k=k, dc=dc, n=n, pb=pb: e.matmul(pb, lhsT=WIN[:, k, 1024 + dc * 128:1024 + (dc + 1) * 128], rhs=HNT[:, k, 0:n],
                                                                           start=(k == 0), stop=(k == 7)),
                             reads=hkeys + wk(1024 + dc * 128, 128, k), writes=[("ps", bb_)])
                    TA, kta = f2()
                    TB, ktb = f2()
                    P.op("act", lambda e, n=n, pa=pa, TA=TA: e.activation(out=TA[:, 0:n], in_=pa, func=AF.Tanh, scale=0.5), reads=[("ps", ba)], writes=[kta])
                    P.op("act", lambda e, n=n, pb=pb, TB=TB: e.activation(out=TB[:, 0:n], in_=pb, func=AF.Tanh, scale=0.5), reads=[("ps", bb_)], writes=[ktb])
                    P.op("dve", lambda e, n=n, TA=TA, YA=YA: e.scalar_tensor_tensor(out=TA[:, 0:n], in0=TA[:, 0:n], scalar=1.0, in1=YA[:, 0:n],
                                                                                  op0=ALU.add, op1=ALU.mult), reads=[kta, kya], writes=[kta])
                    P.op("dve", lambda e, n=n, TB=TB, YB=YB: e.scalar_tensor_tensor(out=TB[:, 0:n], in0=TB[:, 0:n], scalar=1.0, in1=YB[:, 0:n],
                                                                                  op0=ALU.add, op1=ALU.mult), reads=[ktb, kyb], writes=[ktb])
                    P.op("pool", lambda e, dc=dc, n=n, TA=TA, TB=TB: e.tensor_tensor(out=MIX[:, dc, 0:n], in0=TA[:, 0:n], in1=TB[:, 0:n], op=ALU.add),
                         reads=[kta, ktb], writes=[("OA", dc)])
                mkeys = [("OA", dc) for dc in range(8)]
                for i, (t0, rows) in enumerate(tiles):
                    PY = PSB[2] if i % 2 == 0 else PSB[3]
                    pb0 = 4 if i % 2 == 0 else 6
                    HT, kht = f4()
                    HTf = HT[:].rearrange("p a b -> p (a b)")
                    src, skeys = hsrc(l, t0, rows)
                    P.dma("sp", lambda e, rows=rows, src=src, HTf=HTf: e.dma_start(out=HTf[0:rows, :], in_=src), reads=skeys, writes=[kht])
                    for half in range(2):
                        for k in range(8):
                            P.op("pe", lambda e, half=half, k=k, i=i, rows=rows, PY=PY: e.matmul(
                                PY[0:rows, half, :], lhsT=MIX[:, k, i * 128:i * 128 + rows], rhs=W2[:, k, half * 512:(half + 1) * 512],
                                start=(k == 0), stop=(k == 7)), reads=mkeys + [("W2", k)], writes=[("ps", pb0 + half)])
                    HNW, khw = f4()
                    P.op("dve", lambda e, rows=rows, HNW=HNW, HT=HT, PY=PY: e.scalar_tensor_tensor(out=HNW[0:rows], in0=PY[0:rows], scalar=0.5, in1=HT[0:rows],
                                                                                                 op0=ALU.mult, op1=ALU.add),
                         reads=[("ps", pb0), ("ps", pb0 + 1), kht], writes=[khw])
                    HNWf = HNW[:].rearrange("p a b -> p (a b)")
                    if not last:
                        P.dma("pool", lambda e, rows=rows, t0=t0, HNWf=HNWf: e.dma_start(out=hbuf[t0:t0 + rows, :], in_=HNWf[0:rows, :]),
                              reads=[khw], writes=[("hbuf", t0 // 128)], key=khw)
                    else:
                        OT, kot = f4()
                        OTf = OT[:].rearrange("p a b -> p (a b)")
                        ssq, kss = stcol()
                        P.op("pool", lambda e, rows=rows, ssq=ssq: e.memset(ssq[0:rows, :], 0.0), writes=[kss])
                        P.op("act", lambda e, rows=rows, OTf=OTf, HNWf=HNWf, ssq=ssq: e.activation(out=OTf[0:rows, :], in_=HNWf[0:rows, :], func=AF.Square,
                                                                                                   accum_out=ssq[0:rows, :]), reads=[khw], writes=[kot, kss])
                        rt, krt = stcol()
                        P.op("act", lambda e, rows=rows, rt=rt, ssq=ssq: e.activation(out=rt[0:rows, :], in_=ssq[0:rows, :], func=AF.Sqrt, bias=EPS, scale=1.0 / DM),
                             reads=[kss], writes=[krt])
                        rs, krs = stcol()
                        P.op("dve", lambda e, rows=rows, rs=rs, rt=rt: e.reciprocal(out=rs[0:rows, :], in_=rt[0:rows, :]), reads=[krt], writes=[krs])
                        P.op("dve", lambda e, rows=rows, OTf=OTf, HNWf=HNWf, rs=rs: e.scalar_tensor_tensor(
                            out=OTf[0:rows, :], in0=HNWf[0:rows, :], scalar=rs[0:rows, :], in1=GV[0:rows, :], op0=ALU.mult, op1=ALU.mult),
                            reads=[khw, krs, "GV"], writes=[kot])
                        P.dma("pool", lambda e, rows=rows, t0=t0, OTf=OTf: e.dma_start(out=out[t0:t0 + rows, :], in_=OTf[0:rows, :]),
                              reads=[kot], writes=[("out", t0 // 128)], key=kot)
        P.emit(nc)
    return nc, P


def _host_consts(na_rpb):
    W = 64
    c = np.arange(W)
    cs = np.clip(c - 8, 0, W - 16)
    ck = c[:, None]
    cq = c[None, :]
    valid = (ck >= cs[None, :]) & (ck < cs[None, :] + 16)
    bidx = np.clip(ck - cq + 15, 0, 30)
    mt = np.full((DEPTH, 16, 64, 8, 64), NEG, np.float32)
    for l in range(DEPTH):
        for i in range(15):
            g = na_rpb[l][:, i, :][:, bidx]
            g = np.where(valid[None], g, np.float32(NEG))
            mt[l, i] = g.transpose(1, 0, 2)
    mt = mt.reshape(DEPTH, 16, 64, 4, 2, 64).transpose(0, 1, 2, 4, 3, 5)
    return np.ascontiguousarray(mt)


def _rope_tables():
    t = np.arange(SEQ)
    row = np.concatenate([t // 64, np.zeros(NMETA, np.int64)]).astype(np.float32)
    col = np.concatenate([t % 64, np.zeros(NMETA, np.int64)]).astype(np.float32)
    inv = (np.float32(10000.0) ** (-np.arange(0, 32, 2, dtype=np.float32) / np.float32(32))).astype(np.float32)
    ar = row[:, None] * inv[None]
    ac = col[:, None] * inv[None]
    ang = np.concatenate([ar, ar, ac, ac], axis=-1).astype(np.float32)
    cos = np.cos(ang).astype(np.float32).T
    sin = np.sin(ang).astype(np.float32).T
    return np.ascontiguousarray(np.tile(cos, (2, 1))), np.ascontiguousarray(np.tile(sin, (2, 1)))


def _rot_mats():
    R = np.zeros((64, 64), np.float32)
    for d in range(16):
        R[d, d + 16] = -1.0
        R[d + 16, d] = 1.0
        R[32 + d, 48 + d] = -1.0
        R[48 + d, 32 + d] = 1.0
    lhsT = np.zeros((128, 128), np.float32)
    lhsT[:64, :64] = R.T
    lhsT[64:, 64:] = R.T
    bo = np.zeros((128, 128), np.float32)
    bo[:64, :64] = 1.0 / 64
    bo[64:, 64:] = 1.0 / 64
    return lhsT, bo


def _vpat():
    p = np.zeros((128, 194), np.float32)
    p[:, 64] = 1.0
    p[:, 66] = 1.0
    return p


_CACHE = {}


def kernel(x, meta_tokens, norm_g, w_in, q_norm_g, k_norm_g, na_rpb, w_o_attn, w_o_na, w_out, final_norm_g):
    x = np.asarray(x, np.float32)
    f = lambda a: np.ascontiguousarray(np.asarray(a, np.float32))
    if "nc" not in _CACHE:
        _CACHE["nc"] = build_nc()[0]
    nc = _CACHE["nc"]
    cosT, sinT = _rope_tables()
    rmat, bones = _rot_mats()
    shared = {
        "meta": f(meta_tokens),
        "ng": np.ascontiguousarray(np.broadcast_to(f(norm_g)[:, None, :], (DEPTH, 128, DM))),
        "w_in": f(w_in),
        "qg": np.ascontiguousarray(np.tile(f(q_norm_g), (1, 2))[:, :, None]),
        "kg": np.ascontiguousarray(np.tile(f(k_norm_g), (1, 2))[:, :, None]),
        "mtab": _host_consts(f(na_rpb)),
        "w_oa": f(w_o_attn), "w_ob": f(w_o_na), "w_out": f(w_out),
        "fg": np.ascontiguousarray(np.broadcast_to(f(final_norm_g)[None, :], (128, DM))),
        "cosT": cosT, "sinT": sinT, "rmat": rmat, "bones": bones, "eye": np.eye(128, dtype=np.float32),
        "vpat": _vpat(),
    }
    in_maps = []
    for b in range(NCORES):
        m = dict(shared)
        m["x"] = np.ascontiguousarray(x[b])
        in_maps.append(m)
    res = run_bass_kernel_spmd(nc, in_maps, core_ids=list(range(NCORES)))
    return np.stack([np.asarray(res.results[b]["out"], np.float32) for b in range(NCORES)], axis=0)
```

```python
import contextlib
import numpy as np
import concourse.bass as bass
import concourse.mybir as mybir
from concourse.bass_utils import run_bass_kernel_spmd

F32 = mybir.dt.float32
BF16 = mybir.dt.bfloat16
AF = mybir.ActivationFunctionType
ALU = mybir.AluOpType

NCORES = 8
DM = 1024
SEQ = 4096
NMETA = 16
LT = SEQ + NMETA
DEPTH = 2
INC = 5376
EPS = 1e-6
NEG = -30000.0
C_QA, C_KA, C_VA, C_ZA, C_QB, C_KB, C_VB, C_ZB, C_GA, C_GB = 0, 512, 640, 768, 1280, 1792, 2304, 2816, 3328, 4352
ENGS = ("pe", "act", "dve", "pool", "sp")
NRING = 6


class _Op:
    __slots__ = ("eng", "fn", "waits", "sig", "is_dma", "key", "idx", "ph")

    def __init__(self, eng, fn, is_dma, key):
        self.eng, self.fn, self.is_dma, self.key = eng, fn, is_dma, key
        self.waits = []
        self.sig = None


class Prog:
    def __init__(self):
        self.ops = []
        self.last_w = {}
        self.readers = {}
        self.deps = []
        self.phase = ""

    def _add(self, eng, fn, reads, writes, is_dma, key):
        o = _Op(eng, fn, is_dma, key)
        o.idx = len(self.ops)
        o.ph = self.phase
        self.ops.append(o)
        prods = set()
        for r in reads:
            w = self.last_w.get(r)
            if w is not None:
                prods.add(w)
        for r in writes:
            w = self.last_w.get(r)
            if w is not None:
                prods.add(w)
            prods.update(self.readers.get(r, ()))
        prods.discard(o)
        for p in prods:
            if (not p.is_dma) and (not is_dma) and p.eng == "pe" and eng == "pe":
                continue
            self.deps.append((p, o))
        for r in reads:
            self.readers.setdefault(r, []).append(o)
        for r in writes:
            self.last_w[r] = o
            self.readers[r] = []
        return o

    def op(self, eng, fn, reads=(), writes=()):
        return self._add(eng, fn, tuple(reads), tuple(writes), False, None)

    def dma(self, eng, fn, reads=(), writes=(), key=None):
        if key is None:
            key = writes[0]
        return self._add(eng, fn, tuple(reads), tuple(writes), True, ("dma", key))

    def emit(self, nc):
        prod_set = set(p for p, _ in self.deps)
        counters = {}
        hist = {}
        for o in self.ops:
            if o in prod_set or o.is_dma:
                sk = o.key if o.is_dma else ("eng", o.eng)
                counters[sk] = counters.get(sk, 0) + (16 if o.is_dma else 1)
                o.sig = (sk, counters[sk])
                if o.is_dma:
                    hist.setdefault(sk, []).append((o.idx, counters[sk]))
        by_cons = {}
        for p, c in self.deps:
            by_cons.setdefault(c, []).append(p)
        waited = {e: {} for e in ENGS}
        for o in self.ops:
            need = {}
            for p in by_cons.get(o, ()):
                sk, v = p.sig
                if p.is_dma:
                    for (ii, vv) in hist[sk]:
                        if ii < o.idx:
                            v = max(v, vv)
                        else:
                            break
                if v > need.get(sk, 0):
                    need[sk] = v
            for sk, v in need.items():
                if waited[o.eng].get(sk, 0) >= v:
                    continue
                waited[o.eng][sk] = v
                o.waits.append((sk, v))
        semkeys = list(counters.keys())
        with contextlib.ExitStack() as st:
            sems = {}
            for i, sk in enumerate(semkeys):
                sems[sk] = st.enter_context(nc.semaphore("s%d" % i))
            block = st.enter_context(nc.Block())
            per = {e: [o for o in self.ops if o.eng == e] for e in ENGS}

            def run(eng_obj, lst):
                for o in lst:
                    for sk, v in o.waits:
                        eng_obj.wait_ge(sems[sk], v)
                    ins = o.fn(eng_obj)
                    if o.sig is not None:
                        ins.then_inc(sems[o.sig[0]], 16 if o.is_dma else 1)

            @block.tensor
            def _(e):
                run(e, per["pe"])

            @block.scalar
            def _(e):
                run(e, per["act"])

            @block.vector
            def _(e):
                run(e, per["dve"])

            @block.gpsimd
            def _(e):
                run(e, per["pool"])

            @block.sync
            def _(e):
                run(e, per["sp"])
                for sk in semkeys:
                    if sk[0] == "dma":
                        e.wait_ge(sems[sk], counters[sk])


def _groups():
    gs = []
    for g in range(8):
        gs.append((512 * g, 512, [(512 * g + 128 * i, 128) for i in range(4)]))
    gs.append((SEQ, NMETA, [(SEQ, NMETA)]))
    return gs


def build_nc(depth=DEPTH, debug=False):
    nc = bass.Bass("TRN2", target_bir_lowering=False)

    def din(name, shape):
        return nc.dram_tensor(name, shape, F32, kind="ExternalInput").ap()

    x = din("x", [SEQ, DM])
    meta = din("meta", [NMETA, DM])
    ng = din("ng", [DEPTH, 128, DM])
    w_in = din("w_in", [DEPTH, DM, INC])
    qg = din("qg", [DEPTH, 128, 1])
    kg = din("kg", [DEPTH, 128, 1])
    mtab = din("mtab", [DEPTH, 16, 64, 2, 4, 64])
    w_oa = din("w_oa", [DEPTH, 512, DM])
    w_ob = din("w_ob", [DEPTH, 512, DM])
    w_out = din("w_out", [DEPTH, DM, DM])
    fg = din("fg", [128, DM])
    cosT = din("cosT", [128, LT])
    sinT = din("sinT", [128, LT])
    rmat = din("rmat", [128, 128])
    bones = din("bones", [128, 128])
    eye = din("eye", [128, 128])
    vpat = din("vpat", [128, 194])
    out = nc.dram_tensor("out", [SEQ, DM], F32, kind="ExternalOutput").ap()
    hbuf = nc.dram_tensor("hbuf", [LT, DM], F32).ap()
    hnts = nc.dram_tensor("hnts", [128, 8, LT], BF16).ap()
    kbs = nc.dram_tensor("kbs", [128, 4, LT], BF16).ap()
    vbs = nc.dram_tensor("vbs", [LT, 512], BF16).ap()
    dk = "ExternalOutput" if debug else "Internal"
    yas = nc.dram_tensor("yas", [128, 8, LT], F32, kind=dk).ap()
    ybs = nc.dram_tensor("ybs", [128, 8, LT], F32, kind=dk).ap()

    P = Prog()
    dbg_list = []

    def dbg(name, ap, reads, dt):
        if not debug:
            return
        shape = list(ap.shape)
        d = nc.dram_tensor("dbg_" + name, shape, dt, kind="ExternalOutput").ap()
        P.dma("sp", lambda e: e.dma_start(out=d, in_=ap), reads=reads, writes=[("dbg", name)], key=("dbg", len(dbg_list) % 4))
        dbg_list.append(name)

    with contextlib.ExitStack() as st:
        def sb(name, shape, dt):
            return st.enter_context(nc.sbuf_tensor(name, shape, dt))

        def ps(name, shape, dt):
            return st.enter_context(nc.psum_tensor(name, shape, dt))

        WIN = sb("WIN", [128, 8, 2048], BF16)
        W2 = sb("W2", [128, 8, 1024], BF16)
        KAT = sb("KAT", [128, LT], BF16)
        VA1 = sb("VA1", [128, 33, 194], BF16)
        BI = sb("BI", [128, 5, 2, 4, 128], BF16)
        BB = sb("BB", [128, 2, 2, 4, 128], BF16)
        KBR = sb("KBR", [128, 4, NRING * 128], BF16)
        VBR = sb("VBR", [128, NRING, 4, 194], BF16)
        KBM = sb("KBM", [128, 4, 16], BF16)
        VBM = sb("VBM", [16, 4, 194], BF16)
        HN = [sb("HN%d" % i, [128, DM], BF16) for i in range(2)]
        HNT = sb("HNT", [128, 8, 512], BF16)
        QAT = sb("QAT", [128, 4, 512], BF16)
        QBT = sb("QBT", [128, 4, 512], BF16)
        PTS = [sb("PTS%d" % i, [128, 2, 512], BF16) for i in range(3)]
        NF2 = 14
        COSB = sb("COSB", [128, 512], F32)
        SINB = sb("SINB", [128, 512], F32)
        F2 = [sb("F2_%d" % i, [128, 512], F32) for i in range(NF2)]
        NF4 = 4
        F4 = [sb("F4_%d" % i, [128, 2, 512], F32) for i in range(NF4)]
        OA = sb("OA", [128, 8, 512], BF16)
        OB = sb("OB", [128, 4, 512], BF16)
        GV = sb("GV", [128, DM], F32)
        IDB = sb("IDB", [128, 128], BF16)
        BON = sb("BON", [128, 128], F32)
        RMT = sb("RMT", [128, 128], F32)
        SEL = sb("SEL", [128, 64], F32)
        DMY = sb("DMY", [1, 8], F32)
        PATT = sb("PATT", [128, 194], BF16)
        IDF = sb("IDF", [128, 128], F32)
        RTS = [sb("RTS%d" % i, [128, 8], F32) for i in range(2)]
        QG = sb("QG", [128, 1], F32)
        KG = sb("KG", [128, 1], F32)
        ST = sb("ST", [128, 16], F32)
        PSB = [ps("PS%d" % i, [128, 2, 512], F32) for i in range(4)]
        PD = PSB[3][:, 0, :]
        PT = PSB[3][:, 1, :].bitcast(BF16).rearrange("p (a b) -> p a b", a=8)

        def bank(i):
            return PSB[i // 2][:, i % 2, :]

        ctr = {"f2": 0, "f4": 0, "hn": 0, "pts": 0, "st": 0, "bb": 0, "rts": 0}

        def rts():
            i = ctr["rts"] % 2
            ctr["rts"] += 1
            return RTS[i], ("RTS", i)


        def f2():
            i = ctr["f2"] % NF2
            ctr["f2"] += 1
            return F2[i], ("F2", i)

        def f4():
            i = ctr["f4"] % NF4
            ctr["f4"] += 1
            return F4[i], ("F4", i)

        def stcol():
            i = ctr["st"] % 16
            ctr["st"] += 1
            return ST[:, i:i + 1], ("ST", i)

        P.dma("pool", lambda e: e.dma_start(out=IDB[:], in_=eye), writes=["IDB"])
        P.dma("sp", lambda e: e.dma_start(out=BON[:], in_=bones), writes=["BON"])
        P.dma("sp", lambda e: e.dma_start(out=IDF[:], in_=eye), writes=["IDF"])
        P.dma("sp", lambda e: e.dma_start(out=RMT[:], in_=rmat), writes=["RMT"])
        P.op("pool", lambda e: e.memset(SEL[:], 0.0), writes=["SEL"])
        P.op("pool", lambda e: e.memset(SEL[64:65, :], 1.0), writes=["SEL"])
        P.dma("pool", lambda e: e.dma_start(out=PATT[:], in_=vpat), writes=["PATT"])
        P.op("dve", lambda e: e.tensor_copy(out=VA1[:], in_=PATT[:, :].unsqueeze(1).to_broadcast([128, 33, 194])), reads=["PATT"], writes=["VA1i"])
        P.op("dve", lambda e: e.tensor_copy(out=VBR[:].rearrange("p a b c -> p (a b) c"),
                                            in_=PATT[:, :].unsqueeze(1).to_broadcast([128, NRING * 4, 194])), reads=["PATT"], writes=["VBRi"])
        P.op("dve", lambda e: e.tensor_copy(out=VBM[:], in_=PATT[0:16, :].unsqueeze(1).to_broadcast([16, 4, 194])), reads=["PATT"], writes=["VBMi"])
        va_keys = [("VA1", t) for t in range(33)] + [("VA1b", t) for t in range(33)]
        ring_v = [("VBR", s, par) for s in range(NRING) for par in range(2)]
        for k in va_keys:
            P.last_w[k] = P.last_w["VA1i"]
        for k in ring_v:
            P.last_w[k] = P.last_w["VBRi"]
        P.last_w["VBM"] = P.last_w["VBMi"]

        groups = _groups()

        def wk(d0, ncols, k):
            return [("WIN", b, k) for b in range(d0 // 128, (d0 + ncols - 1) // 128 + 1)]

        def wka(d0, ncols):
            return [x for k in range(8) for x in wk(d0, ncols, k)]

        def load_w_cols(dst3, l, c0, ncols, d0, key, semname="w"):
            src = w_in[l].rearrange("(k p) c -> p k c", p=128)
            for k in range(8):
                P.dma("pool", lambda e, k=k: e.dma_start(out=dst3[:, k, d0:d0 + ncols], in_=src[:, k, c0:c0 + ncols]),
                      writes=wk(d0, ncols, k), key=(semname, k))

        def hsrc(l, t0, rows):
            if l == 0:
                if t0 >= SEQ:
                    return meta[0:rows, :], ()
                return x[t0:t0 + rows, :], ()
            return hbuf[t0:t0 + rows, :], [("hbuf", t0 // 128)]

        def normrope_gen(src, src_keys, gvec, gkey, dst, dst_keys, n, COS, SIN, cs_keys, bq=0, br=1):
            SQ, ksq = f2()
            P.op("act", lambda e: e.activation(out=SQ[:, 0:n], in_=src, func=AF.Square), reads=src_keys, writes=[ksq])
            yield
            pq = bank(bq)[:, 0:n]
            P.op("pe", lambda e: e.matmul(pq, lhsT=BON[:], rhs=SQ[:, 0:n], start=True, stop=True),
                 reads=[ksq, "BON"], writes=[("ps", bq)])
            yield
            RT, krt = f2()
            P.op("act", lambda e: e.activation(out=RT[:, 0:n], in_=pq, func=AF.Sqrt, bias=EPS, scale=1.0),
                 reads=[("ps", bq)], writes=[krt])
            yield
            RS, krs = f2()
            P.op("dve", lambda e: e.reciprocal(out=RS[:, 0:n], in_=RT[:, 0:n]), reads=[krt], writes=[krs])
            yield
            QN, kqn = f2()
            P.op("dve", lambda e: e.scalar_tensor_tensor(out=QN[:, 0:n], in0=src, scalar=gvec, in1=RS[:, 0:n],
                                                         op0=ALU.mult, op1=ALU.mult),
                 reads=list(src_keys) + [krs, gkey], writes=[kqn])
            yield
            pr = bank(br)[:, 0:n]
            P.op("pe", lambda e: e.matmul(pr, lhsT=RMT[:], rhs=QN[:, 0:n], start=True, stop=True),
                 reads=[kqn, "RMT"], writes=[("ps", br)])
            A, ka = f2()
            P.op("pool", lambda e: e.tensor_tensor(out=A[:, 0:n], in0=QN[:, 0:n], in1=COS[:, 0:n], op=ALU.mult),
                 reads=[kqn] + cs_keys, writes=[ka])
            yield
            B, kb = f2()
            P.op("dve", lambda e: e.tensor_tensor(out=B[:, 0:n], in0=pr, in1=SIN[:, 0:n], op=ALU.mult),
                 reads=[("ps", br)] + cs_keys, writes=[kb])
            yield
            P.op("pool", lambda e: e.tensor_tensor(out=dst, in0=A[:, 0:n], in1=B[:, 0:n], op=ALU.add),
                 reads=[ka, kb], writes=dst_keys)
            yield

        def run_gens(gens):
            gens = list(gens)
            while gens:
                for g_ in list(gens):
                    try:
                        next(g_)
                    except StopIteration:
                        gens.remove(g_)

        def normrope(*a, **k):
            run_gens([normrope_gen(*a, **k)])

        def load_cs(tok0, n):
            COS, kc, SIN, ks = COSB, "COSB", SINB, "SINB"
            P.dma("sp", lambda e: e.dma_start(out=COS[:, 0:n], in_=cosT[:, tok0:tok0 + n]), writes=[kc])
            P.dma("sp", lambda e: e.dma_start(out=SIN[:, 0:n], in_=sinT[:, tok0:tok0 + n]), writes=[ks])
            return COS, SIN, [kc, ks]

        def rms_tile(l, t0, rows, gkey="GV"):
            HT, kht = f4()
            HTf = HT[:].rearrange("p a b -> p (a b)")
            src, skeys = hsrc(l, t0, rows)
            P.dma("sp", lambda e: e.dma_start(out=HTf[0:rows, :], in_=src), reads=skeys, writes=[kht])
            i = ctr["hn"] % 2
            ctr["hn"] += 1
            HNs, khn = HN[i], ("HN", i)
            ssq, kss = stcol()
            P.op("pool", lambda e: e.memset(ssq[0:rows, :], 0.0), writes=[kss])
            P.op("act", lambda e: e.activation(out=HNs[0:rows, :], in_=HTf[0:rows, :], func=AF.Square,
                                               accum_out=ssq[0:rows, :]), reads=[kht], writes=[khn, kss])
            rt, krt = stcol()
            P.op("act", lambda e: e.activation(out=rt[0:rows, :], in_=ssq[0:rows, :], func=AF.Sqrt, bias=EPS,
                                               scale=1.0 / DM), reads=[kss], writes=[krt])
            rs, krs = stcol()
            P.op("dve", lambda e: e.reciprocal(out=rs[0:rows, :], in_=rt[0:rows, :]), reads=[krt], writes=[krs])
            P.op("dve", lambda e: e.scalar_tensor_tensor(out=HNs[0:rows, :], in0=HTf[0:rows, :], scalar=rs[0:rows, :],
                                                         in1=GV[0:rows, :], op0=ALU.mult, op1=ALU.mult),
                 reads=[kht, krs, gkey], writes=[khn])
            return HTf, kht, HNs, khn

        for l in range(depth):
            last = (l == depth - 1)
            P.phase = "A%d" % l
            P.dma("sp", lambda e, l=l: e.dma_start(out=GV[:], in_=ng[l]), writes=["GV"])
            P.dma("sp", lambda e, l=l: e.dma_start(out=KG[:], in_=kg[l]), writes=["KG"])
            P.dma("sp", lambda e, l=l: e.dma_start(out=QG[:], in_=qg[l]), writes=["QG"])
            load_w_cols(WIN, l, C_KA, 256, 768, "WIN")
            load_w_cols(WIN, l, C_KB, 1024, 1024, "WIN")
            for c in range(4):
                P.dma("pool", lambda e, c=c, l=l: e.dma_start(out=W2[0:64, c, :], in_=w_oa[l, c * 64:(c + 1) * 64, :]), writes=[("W2", c)], key=("w2", c))
                P.dma("pool", lambda e, c=c, l=l: e.dma_start(out=W2[64:128, c, :], in_=w_oa[l, (4 + c) * 64:(5 + c) * 64, :]), writes=[("W2", c)], key=("w2", c))
                P.dma("pool", lambda e, c=c, l=l: e.dma_start(out=W2[:, 4 + c, :], in_=w_ob[l, c * 128:(c + 1) * 128, :]), writes=[("W2", 4 + c)], key=("w2", 4 + c))
            for c in range(4):
                load_w_cols(WIN, l, C_QA + 64 * c, 64, c * 128, "WIN", semname="wp")
                load_w_cols(WIN, l, C_QA + 256 + 64 * c, 64, c * 128 + 64, "WIN", semname="wp")
            for c in range(2):
                load_w_cols(WIN, l, C_ZA + 64 * c, 64, 512 + c * 128, "WIN", semname="wp")
                load_w_cols(WIN, l, C_ZA + 256 + 64 * c, 64, 512 + c * 128 + 64, "WIN", semname="wp")
            HB = [HNT, OA]

            def hk(bf, i):
                return ("HNT", i) if bf == 0 else ("OAH", i)

            oa_alias = [("OA", x_) for x_ in range(8)] + [("OAH", i_) for i_ in range(4)]
            P.op("pool", lambda e: e.memset(DMY[0:1, 0:1], 0.0), writes=oa_alias)

            def stage1_gen(grp, bf):
                tok0, n, tiles = grp
                HBb = HB[bf]

                def evac(i, rows):
                    if i % 2 == 0:
                        P.op("act", lambda e: e.copy(out=HBb[:, :, i * 128:i * 128 + rows], in_=PT[:, :, 0:rows]),
                             reads=[("ps", 7)], writes=[hk(bf, i)])
                    else:
                        P.op("dve", lambda e: e.tensor_copy(out=HBb[:, :, i * 128:i * 128 + rows], in_=PT[:, :, 0:rows]),
                             reads=[("ps", 7)], writes=[hk(bf, i)])

                prev_t = None
                for i, (t0, rows) in enumerate(tiles):
                    HTf, kht, HNs, khn = rms_tile(l, t0, rows)
                    yield
                    if prev_t is not None:
                        evac(*prev_t)
                    for k in range(8):
                        P.op("pe", lambda e, k=k, HNs=HNs, rows=rows: e.transpose(PT[:, k, 0:rows], HNs[0:rows, k * 128:(k + 1) * 128],
                                                                                  IDB[0:rows, 0:rows]),
                             reads=[khn, "IDB"], writes=[("ps", 7)])
                    prev_t = (i, rows)
                    yield
                evac(*prev_t)
                yield

            def proj_gen(grp, bf):
                tok0, n, tiles = grp
                HBb = HB[bf]
                hkeys = [hk(bf, i) for i in range(len(tiles))]
                tkeys = [t0 // 128 for (t0, _) in tiles]
                COS, SIN, cskeys = load_cs(tok0, n)
                P.dma("pool", lambda e: e.dma_start(out=hnts[:, :, tok0:tok0 + n], in_=HBb[:, :, 0:n]),
                      reads=hkeys, writes=[("hnts", t) for t in tkeys], key=("HNTst", bf))
                pk = bank(6)[:, 0:n]
                for k in range(8):
                    P.op("pe", lambda e, k=k: e.matmul(pk, lhsT=WIN[:, k, 768:896], rhs=HBb[:, k, 0:n], start=(k == 0), stop=(k == 7)),
                         reads=hkeys + wk(768, 128, k), writes=[("ps", 6)])
                yield
                nr = normrope_gen(pk, [("ps", 6)], KG[:, 0:1], "KG", KAT[:, tok0:tok0 + n], [("KAT", t) for t in tkeys], n, COS, SIN, cskeys)
                for i, (t0, rows) in enumerate(tiles):
                    pv = bank(2)[0:rows, 0:128]
                    for k in range(8):
                        P.op("pe", lambda e, k=k, i=i, rows=rows, pv=pv: e.matmul(pv, lhsT=HBb[:, k, i * 128:i * 128 + rows], rhs=WIN[:, k, 896:1024],
                                                                               start=(k == 0), stop=(k == 7)),
                             reads=[hk(bf, i)] + wk(896, 128, k), writes=[("ps", 2)])
                    tt = t0 // 128
                    P.op("act", lambda e, tt=tt, rows=rows, pv=pv: e.copy(out=VA1[0:rows, tt, 0:64], in_=pv[:, 0:64]),
                         reads=[("ps", 2)], writes=[("VA1", tt), "ps2rd"])
                    P.op("dve", lambda e, tt=tt, rows=rows, pv=pv: e.tensor_copy(out=VA1[0:rows, tt, 130:194], in_=pv[:, 64:128]),
                         reads=[("ps", 2)], writes=[("VA1b", tt), "ps2rd"])
                    nr = step_gen(nr)
                    yield
                for c in range(4):
                    bk = 3 + (c % 2)
                    pb = bank(bk)[:, 0:n]
                    for k in range(8):
                        P.op("pe", lambda e, k=k, c=c, pb=pb: e.matmul(pb, lhsT=WIN[:, k, 1024 + c * 128:1024 + (c + 1) * 128], rhs=HBb[:, k, 0:n],
                                                                      start=(k == 0), stop=(k == 7)),
                             reads=hkeys + wk(1024 + c * 128, 128, k), writes=[("ps", bk)])
                    P.op("dve", lambda e, c=c, pb=pb: e.tensor_copy(out=QAT[:, c, 0:n], in_=pb), reads=[("ps", bk)], writes=[("QAT", c)])
                    nr = step_gen(nr)
                    yield
                P.dma("pool", lambda e: e.dma_start(out=kbs[:, :, tok0:tok0 + n], in_=QAT[:, :, 0:n]),
                      reads=[("QAT", c) for c in range(4)], writes=[("kbs", t) for t in tkeys], key=("KBst",))
                for i, (t0, rows) in enumerate(tiles):
                    bk = 5 if i % 2 == 0 else 2
                    pv = bank(bk)[0:rows, :]
                    for k in range(8):
                        P.op("pe", lambda e, k=k, i=i, rows=rows, pv=pv: e.matmul(pv, lhsT=HBb[:, k, i * 128:i * 128 + rows], rhs=WIN[:, k, 1536:2048],
                                                                               start=(k == 0), stop=(k == 7)),
                             reads=[hk(bf, i)] + wk(1536, 512, k), writes=[("ps", bk)])
                    P.op("act", lambda e, i=i, rows=rows, pv=pv: e.copy(out=QBT[0:rows, i, :], in_=pv), reads=[("ps", bk)], writes=[("QBT", i)])
                    P.dma("pool", lambda e, i=i, rows=rows, t0=t0: e.dma_start(out=vbs[t0:t0 + rows, :], in_=QBT[0:rows, i, :]),
                          reads=[("QBT", i)], writes=[("vbs", t0 // 128)], key=("VBst", i))
                    nr = step_gen(nr)
                    yield
                while nr is not None:
                    nr = step_gen(nr)
                    yield

            def step_gen(g_):
                if g_ is None:
                    return None
                try:
                    next(g_)
                    return g_
                except StopIteration:
                    return None

            s1 = stage1_gen(groups[0], 0)
            while s1 is not None:
                s1 = step_gen(s1)
            for gi_, grp in enumerate(groups):
                bf = gi_ % 2
                pg = proj_gen(grp, bf)
                s1 = stage1_gen(groups[gi_ + 1], 1 - bf) if gi_ + 1 < len(groups) else None
                while s1 is not None or pg is not None:
                    s1 = step_gen(s1)
                    for _ in range(3):
                        pg = step_gen(pg)
            P.op("pool", lambda e: e.memset(DMY[0:1, 0:1], 0.0), writes=oa_alias)

            if last:
                b_groups = groups[:8]
            else:
                b_groups = groups
            P.phase = "B1w%d" % l
            load_w_cols(WIN, l, C_QB, 512, 1024, "WIN")
            for c in range(2, 4):
                load_w_cols(WIN, l, C_ZA + 64 * c, 64, 512 + c * 128, "WIN")
                load_w_cols(WIN, l, C_ZA + 256 + 64 * c, 64, 512 + c * 128 + 64, "WIN")
            load_w_cols(WIN, l, C_ZB, 512, 1536, "WIN")

            def build_bias(dst, dkey, t, cidx, l=l):
                for rk_l in range(2):
                    for rq_l in range(2):
                        rk = 2 * cidx + rk_l
                        rq = 2 * t + rq_l
                        rs = min(max(rq - 4, 0), 56)
                        idx = (rk - rq + 7) if (rs <= rk <= rs + 7) else 15
                        P.dma("pool", lambda e, rk_l=rk_l, rq_l=rq_l, idx=idx: e.dma_start(
                            out=dst[rk_l * 64:(rk_l + 1) * 64, :, :, rq_l * 64:(rq_l + 1) * 64], in_=mtab[l, idx]),
                            writes=[dkey], key=(dkey, rk_l, rq_l) if dkey[0] == "BB" else ("bias", rk_l, rq_l))

            for j in range(5):
                build_bias(BI[:, j], ("BI", j), 10, 10 + j - 2)
            P.dma("sp", lambda e: e.dma_start(out=KBM[:], in_=kbs[:, :, SEQ:LT]), reads=[("kbs", 32)], writes=["KBM"])
            for par in range(2):
                P.dma("sp", lambda e, par=par: e.dma_start(out=VBM[:, :, 130 * par:130 * par + 64],
                                                            in_=vbs[SEQ:LT, :].rearrange("p (a t d) -> p a t d", a=4, t=2)[:, :, par, :]),
                      reads=[("vbs", 32)], writes=["VBM"], key=("VBM", par))

            def ring_load(cidx):
                s = cidx % NRING
                P.dma("sp", lambda e: e.dma_start(out=KBR[:, :, s * 128:(s + 1) * 128], in_=kbs[:, :, cidx * 128:(cidx + 1) * 128]),
                      reads=[("kbs", cidx)], writes=[("KBR", s)])
                for par in range(2):
                    P.dma("sp", lambda e, par=par: e.dma_start(out=VBR[:, s, :, 130 * par:130 * par + 64],
                                                                in_=vbs[cidx * 128:(cidx + 1) * 128, :].rearrange("p (a t d) -> p a t d", a=4, t=2)[:, :, par, :]),
                          reads=[("vbs", cidx)], writes=[("VBR", s, par)], key=("VBR", s, par))

            ring_next = 0

            def pair_finish(zmm, zper, ncols, blocks, out_fn, out_keys):
                P4, P5, P6 = bank(4), bank(5), bank(6)
                X4, kx4 = f2()
                P.op("act", lambda e: e.copy(out=X4[0:65, 0:ncols], in_=P4[0:65, 0:ncols]), reads=[("ps", 4)], writes=[kx4])
                X5, kx5 = f2()
                P.op("dve", lambda e: e.tensor_copy(out=X5[:, 0:ncols], in_=P5[:, 0:ncols]), reads=[("ps", 5)], writes=[kx5])
                yield
                for zi, zf in enumerate(zmm):
                    zf()
                    if zi % zper == zper - 1:
                        yield
                TH, kth = f2()
                P.op("act", lambda e: e.activation(out=TH[:, 0:ncols], in_=P6[:, 0:ncols], func=AF.Tanh, scale=0.5), reads=[("ps", 6)], writes=[kth])
                ZS, kzs = f2()
                P.op("dve", lambda e: e.scalar_tensor_tensor(out=ZS[:, 0:ncols], in0=TH[:, 0:ncols], scalar=1.0, in1=P6[:, 0:ncols],
                                                             op0=ALU.add, op1=ALU.mult), reads=[kth, ("ps", 6)], writes=[kzs])
                T1, kt1 = f2()
                P.op("dve", lambda e: e.scalar_tensor_tensor(out=T1[0:64, 0:ncols], in0=X4[0:64, 0:ncols], scalar=0.5, in1=ZS[0:64, 0:ncols],
                                                             op0=ALU.mult, op1=ALU.mult), reads=[kx4, kzs], writes=[kt1])
                P.op("dve", lambda e: e.scalar_tensor_tensor(out=T1[64:128, 0:ncols], in0=X5[64:128, 0:ncols], scalar=0.5, in1=ZS[64:128, 0:ncols],
                                                             op0=ALU.mult, op1=ALU.mult), reads=[kx5, kzs, kt1], writes=[kt1])
                yield
                NB = len(blocks)
                for bi_, (c0, w) in enumerate(blocks):
                    P.op("pe", lambda e, bi_=bi_, c0=c0, w=w: e.matmul(P6[0:w, bi_:bi_ + 1], lhsT=X4[0:65, c0:c0 + w], rhs=SEL[0:65, 0:1],
                                                                      start=True, stop=True), reads=[kx4, "SEL"], writes=[("ps", 6)])
                    P.op("pe", lambda e, bi_=bi_, c0=c0, w=w: e.matmul(P6[0:w, NB + bi_:NB + bi_ + 1], lhsT=X5[0:65, c0:c0 + w], rhs=IDF[0:65, 0:1],
                                                                      start=False, stop=True, skip_group_check=True), reads=[kx5, "IDF"], writes=[("ps", 6)])
                yield
                RT, krt = rts()
                P.op("dve", lambda e: e.reciprocal(out=RT[:, 0:2 * NB], in_=P6[:, 0:2 * NB]), reads=[("ps", 6)], writes=[krt])
                RTB, krb = f2()
                RTBv = RTB[:, :].rearrange("p (a b) -> p a b", a=4)
                P.op("dve", lambda e: e.tensor_copy(out=RTBv[:, 0:NB, 0:64], in_=RT[:, 0:NB].unsqueeze(2).to_broadcast([128, NB, 64])),
                     reads=[krt], writes=[krb])
                P.op("dve", lambda e: e.tensor_copy(out=RTBv[:, 0:NB, 64:128], in_=RT[:, NB:2 * NB].unsqueeze(2).to_broadcast([128, NB, 64])),
                     reads=[krt, krb], writes=[krb])
                for bi_, (c0, w) in enumerate(blocks):
                    P.op("pe", lambda e, bi_=bi_, c0=c0, w=w: e.matmul(P6[:, c0:c0 + w], lhsT=RTBv[0:w, bi_, :], rhs=IDF[0:w, 0:w],
                                                                      start=(bi_ == 0), stop=True, skip_group_check=True),
                         reads=[krb, "IDF"], writes=[("ps", 6)])
                yield
                P.op("dve", lambda e: out_fn(e, T1, P6), reads=[kt1, ("ps", 6)], writes=out_keys)
                yield

            def step_gen(g_):
                if g_ is None:
                    return None
                try:
                    next(g_)
                    return g_
                except StopIteration:
                    return None

            def drain_gen(g_):
                while g_ is not None:
                    g_ = step_gen(g_)

            def bg_outproj(which, tok0p, np_, tkp, rkeys_fn):
                SRC = OA if which == 0 else OB
                dst = yas if which == 0 else ybs
                dname = "yas" if which == 0 else "ybs"
                for dc in range(8):
                    py = bank(7)[:, 0:np_]
                    for c in range(4):
                        P.op("pe", lambda e, c=c, dc=dc: e.matmul(py, lhsT=W2[:, 4 * which + c, dc * 128:(dc + 1) * 128], rhs=SRC[:, c, 0:np_],
                                                                 start=(c == 0), stop=(c == 3)),
                             reads=rkeys_fn(c) + [("W2", 4 * which + c)], writes=[("ps", 7)])
                        if c % 2 == 1:
                            yield
                    Y, ky = f2()
                    P.op("dve", lambda e, Y=Y: e.tensor_copy(out=Y[:, 0:np_], in_=py), reads=[("ps", 7)], writes=[ky])
                    P.dma("pool", lambda e, dc=dc, Y=Y: e.dma_start(out=dst[:, dc, tok0p:tok0p + np_], in_=Y[:, 0:np_]),
                          reads=[ky], writes=[(dname, dc, t) for t in tkp], key=ky)
                    yield
                    yield

            def bg_qB(n_, hk):
                for c in range(4):
                    pk = bank(7)[:, 0:n_]
                    for k in range(8):
                        P.op("pe", lambda e, k=k, c=c: e.matmul(pk, lhsT=WIN[:, k, 1024 + c * 128:1024 + (c + 1) * 128], rhs=HNT[:, k, 0:n_],
                                                               start=(k == 0), stop=(k == 7)),
                             reads=hk + wk(1024 + c * 128, 128, k), writes=[("ps", 7)])
                        if k % 2 == 1:
                            yield
                    P.op("dve", lambda e, c=c: e.tensor_scalar(out=QBT[:, c, 0:n_], in0=pk, scalar1=0.125, scalar2=None, op0=ALU.mult),
                         reads=[("ps", 7)], writes=[("QBT", c)])
                    yield
                    yield

            def chain_gens(*gs):
                for g_ in gs:
                    if g_ is not None:
                        yield from g_

            prev_bg = []

            for (tok0, n, tiles) in b_groups:
                P.dma("sp", lambda e, tok0=tok0, n=n: e.dma_start(out=HNT[:, :, 0:n], in_=hnts[:, :, tok0:tok0 + n]),
                      reads=[("hnts", t0 // 128) for (t0, _) in tiles], writes=[("HNT", i) for i in range(4)], key=("HNTld",))
                hkeys = [("HNT", i) for i in range(4)]
                COS, SIN, cskeys = load_cs(tok0, n)
                P.phase = "qA%d" % l
                for c0 in (0, 2):
                    gens = []
                    for ci_, c in enumerate((c0, c0 + 1)):
                        bk = 6 + ci_
                        pk = bank(bk)[:, 0:n]
                        for k in range(8):
                            P.op("pe", lambda e, k=k, c=c, pk=pk, n=n: e.matmul(pk, lhsT=WIN[:, k, c * 128:(c + 1) * 128], rhs=HNT[:, k, 0:n],
                                                                       start=(k == 0), stop=(k == 7)),
                                 reads=hkeys + wk(c * 128, 128, k), writes=[("ps", bk)])
                        gens.append(normrope_gen(pk, [("ps", bk)], QG[:, 0:1], "QG", QAT[:, c, 0:n], [("QAT", c)], n, COS, SIN, cskeys,
                                                 bq=2 * ci_, br=2 * ci_ + 1))
                    run_gens(gens)
                P.phase = "GQA%d" % l
                P.phase = "GQA%d" % l
                PC = PSB[2]
                pending = []

                def gqa_S(c, kt, n=n):
                    k0 = kt * 128
                    kn = 128 if kt < 32 else NMETA
                    PSs = PSB[kt % 2]
                    skeys = [("ps", 2 * (kt % 2)), ("ps", 2 * (kt % 2) + 1)]
                    for j in range(2):
                        P.op("pe", lambda e, j=j: e.matmul(
                            PSs[0:kn, j, 0:n], lhsT=KAT[64 * j:64 * j + 64, k0:k0 + kn], rhs=QAT[64 * j:64 * j + 64, c, 0:n], start=True, stop=True),
                            reads=[("KAT", kt), ("QAT", c)], writes=[skeys[j]])

                def gqa_E(c, kt, n=n):
                    kn = 128 if kt < 32 else NMETA
                    PSs = PSB[kt % 2]
                    skeys = [("ps", 2 * (kt % 2)), ("ps", 2 * (kt % 2) + 1)]
                    pi = ctr["pts"] % 3
                    ctr["pts"] += 1
                    PTs, kpt = PTS[pi], ("PTS", pi)
                    P.op("act", lambda e: e.activation(out=PTs[0:kn, :, 0:n], in_=PSs[0:kn, :, 0:n], func=AF.Exp, scale=0.125),
                         reads=skeys, writes=[kpt])
                    return PTs, kpt

                def gqa_PV(c, kt, PTs, kpt, n=n):
                    kn = 128 if kt < 32 else NMETA
                    P.op("pe", lambda e: e.matmul(bank(4)[0:65, 0:n], lhsT=VA1[0:kn, kt, 0:65], rhs=PTs[0:kn, 0, 0:n], start=(kt == 0), stop=(kt == 32)),
                         reads=[("VA1", kt), kpt], writes=[("ps", 4)])
                    P.op("pe", lambda e: e.matmul(bank(5)[:, 0:n], lhsT=VA1[0:kn, kt, 66:194], rhs=PTs[0:kn, 1, 0:n], start=(kt == 0), stop=(kt == 32)),
                         reads=[("VA1b", kt), kpt], writes=[("ps", 5)])

                def gqa_finish(c, n=n):
                    P6 = bank(6)
                    zmm = []
                    for k in range(8):
                        zmm.append(lambda k=k: P.op("pe", lambda e: e.matmul(P6[:, 0:n], lhsT=WIN[:, k, 512 + c * 128:512 + (c + 1) * 128], rhs=HNT[:, k, 0:n],
                                                                            start=(k == 0), stop=(k == 7)),
                                                    reads=hkeys + wk(512 + c * 128, 128, k), writes=[("ps", 6)]))
                    nb = (n + 127) // 128
                    blocks = [(qb * 128, min(128, n - qb * 128)) for qb in range(nb)]
                    return pair_finish(zmm, 2, n, blocks,
                                       lambda e, T1, BC: e.tensor_tensor(out=OA[:, c, 0:n], in0=T1[:, 0:n], in1=BC[:, 0:n], op=ALU.mult),
                                       [("OA", c)])

                bg_rest = chain_gens(prev_bg[1] if prev_bg else None, bg_qB(n, hkeys))
                prev_bg = []
                fin = None
                for c in range(4):
                    gqa_S(c, 0)
                    gqa_S(c, 1)
                    fin = step_gen(fin)
                    for kt in range(33):
                        PTs_, kpt_ = gqa_E(c, kt)
                        if kt + 2 < 33:
                            gqa_S(c, kt + 2)
                        gqa_PV(c, kt, PTs_, kpt_)
                        fin = step_gen(fin)
                        bg_rest = step_gen(bg_rest)
                    drain_gen(fin)
                    fin = gqa_finish(c)
                is_last_grp = (tok0, n) == (b_groups[-1][0], b_groups[-1][1])
                if is_last_grp:
                    drain_gen(fin)
                    fin = None
                carry = fin
                drain_gen(bg_rest)
                if (tok0, n) == (b_groups[-1][0], b_groups[-1][1]):
                    load_w_cols(WIN, l, C_GA, 512, 0, "WIN", semname="wp")
                    load_w_cols(WIN, l, C_GB, 512, 1024, "WIN", semname="wp")
                    load_w_cols(WIN, l, C_GA + 512, 512, 512, "WIN", semname="wp")
                tkeys = [t0 // 128 for (t0, _) in tiles]
                P.phase = "NA%d" % l
                fin = carry
                carry_active = carry is not None
                bg_na = bg_outproj(0, tok0, n, tkeys, lambda c_: [("OA", c_)])
                for i, (t0, rows) in enumerate(tiles):
                    is_meta = (t0 >= SEQ)
                    t = t0 // 128
                    chunks = []
                    interior = False
                    if not is_meta:
                        r0, r1 = 2 * t, 2 * t + 1
                        lo = min(max(r0 - 4, 0), 56) // 2
                        hi = (min(max(r1 - 4, 0), 56) + 7) // 2
                        want = min(max(t + 3, hi), 31)
                        while ring_next <= want:
                            ring_load(ring_next)
                            ring_next += 1
                        interior = (2 <= t <= 29)
                        chunks = list(range(lo, hi + 1))
                    nch = len(chunks) + 1
                    q0 = i * 128

                    def na_S(ci, chunks=chunks, t=t, rows=rows, q0=q0, interior=interior):
                        PSs = PSB[ci % 2]
                        skeys = [("ps", 2 * (ci % 2)), ("ps", 2 * (ci % 2) + 1)]
                        if ci < len(chunks):
                            cidx = chunks[ci]
                            s_ = cidx % NRING
                            if interior:
                                BT, kbt = BI[:, cidx - t + 2], ("BI", cidx - t + 2)
                            else:
                                bi = ctr["bb"] % 2
                                ctr["bb"] += 1
                                BT, kbt = BB[:, bi], ("BB", bi)
                                build_bias(BT, kbt, t, cidx)
                            for par in range(2):
                                P.op("pe", lambda e, par=par: e.matmul(PSs[:, par, :], lhsT=IDB[:], rhs=BT[:, par].rearrange("p a b -> p (a b)"),
                                                                       start=True, stop=False, skip_group_check=True),
                                     reads=[kbt, "IDB"], writes=[skeys[par]])
                            for p4 in range(4):
                                for par in range(2):
                                    P.op("pe", lambda e, par=par, p4=p4: e.matmul(
                                        PSs[:, par, p4 * 128:p4 * 128 + rows], lhsT=KBR[64 * par:64 * par + 64, p4, s_ * 128:(s_ + 1) * 128],
                                        rhs=QBT[64 * par:64 * par + 64, p4, q0:q0 + rows], start=False, stop=True, skip_group_check=True),
                                        reads=[("KBR", s_), ("QBT", p4)], writes=[skeys[par]])
                        else:
                            for p4 in range(4):
                                for par in range(2):
                                    P.op("pe", lambda e, par=par, p4=p4: e.matmul(
                                        PSs[0:16, par, p4 * 128:p4 * 128 + rows], lhsT=KBM[64 * par:64 * par + 64, p4, :],
                                        rhs=QBT[64 * par:64 * par + 64, p4, q0:q0 + rows], start=(p4 == 0), stop=True, skip_group_check=True),
                                        reads=["KBM", ("QBT", p4)], writes=[skeys[par]])

                    def na_E(ci, chunks=chunks, rows=rows):
                        PSs = PSB[ci % 2]
                        skeys = [("ps", 2 * (ci % 2)), ("ps", 2 * (ci % 2) + 1)]
                        pi = ctr["pts"] % 3
                        ctr["pts"] += 1
                        PTs, kpt = PTS[pi], ("PTS", pi)
                        if ci != len(chunks):
                            P.op("act", lambda e: e.activation(out=PTs[:, :, :], in_=PSs[:, :, :], func=AF.Exp), reads=skeys, writes=[kpt])
                        else:
                            P.op("act", lambda e: e.activation(
                                out=PTs[0:16].rearrange("p a (b c) -> p a b c", b=4)[:, :, :, 0:rows],
                                in_=PSs[0:16].rearrange("p a (b c) -> p a b c", b=4)[:, :, :, 0:rows], func=AF.Exp),
                                reads=skeys, writes=[kpt])
                        return PTs, kpt

                    def na_PV(ci, PTs, kpt, chunks=chunks, rows=rows):
                        last_c = (ci == len(chunks))
                        s_ = (chunks[ci] % NRING) if not last_c else 0
                        for p4 in range(4):
                            for par in range(2):
                                if par == 0:
                                    outp = bank(4)[0:65, p4 * 128:p4 * 128 + rows]
                                    lo, hi = 0, 65
                                else:
                                    outp = bank(5)[:, p4 * 128:p4 * 128 + rows]
                                    lo, hi = 66, 194
                                if not last_c:
                                    P.op("pe", lambda e, par=par, p4=p4, outp=outp, lo=lo, hi=hi: e.matmul(
                                        outp, lhsT=VBR[:, s_, p4, lo:hi], rhs=PTs[:, par, p4 * 128:p4 * 128 + rows],
                                        start=(ci == 0 and p4 == 0), stop=False, skip_group_check=True),
                                        reads=[("VBR", s_, par), kpt], writes=[("ps", 4 + par)])
                                else:
                                    P.op("pe", lambda e, par=par, p4=p4, outp=outp, lo=lo, hi=hi: e.matmul(
                                        outp, lhsT=VBM[0:16, p4, lo:hi], rhs=PTs[0:16, par, p4 * 128:p4 * 128 + rows],
                                        start=(ci == 0 and p4 == 0), stop=True, skip_group_check=True),
                                        reads=["VBM", kpt], writes=[("ps", 4 + par)])

                    def na_finish(i=i, rows=rows, q0=q0):
                        P6 = bank(6)
                        zmm = []
                        for p4 in range(4):
                            for k in range(8):
                                zmm.append(lambda p4=p4, k=k: P.op("pe", lambda e: e.matmul(
                                    P6[:, p4 * 128:p4 * 128 + rows], lhsT=WIN[:, k, 1536 + p4 * 128:1536 + (p4 + 1) * 128],
                                    rhs=HNT[:, k, q0:q0 + rows], start=(k == 0), stop=(k == 7), skip_group_check=True),
                                    reads=[("HNT", i)] + wk(1536 + p4 * 128, 128, k), writes=[("ps", 6)]))
                        blocks = [(p4 * 128, rows) for p4 in range(4)]
                        return pair_finish(zmm, 32, 512, blocks,
                                           lambda e, T1, BC: e.tensor_tensor(
                                               out=OB[:, :, q0:q0 + rows], in0=T1[:, :].rearrange("p (a b) -> p a b", a=4)[:, :, 0:rows],
                                               in1=BC[:, :].rearrange("p (a b) -> p a b", a=4)[:, :, 0:rows], op=ALU.mult),
                                           [("OB", i)])

                    na_S(0)
                    if nch > 1:
                        na_S(1)
                    fin = step_gen(fin)
                    for ci in range(nch):
                        PTs_, kpt_ = na_E(ci)
                        if ci + 2 < nch:
                            na_S(ci + 2)
                        na_PV(ci, PTs_, kpt_)
                        fin = step_gen(fin)
                        if fin is None:
                            carry_active = False
                        if not carry_active:
                            bg_na = step_gen(bg_na)
                            bg_na = step_gen(bg_na)
                    drain_gen(fin)
                    carry_active = False
                    fin = na_finish()
                drain_gen(fin)
                drain_gen(bg_na)
                obk = [("OB", i_) for i_ in range(len(tiles))]
                prev_bg = [None, bg_outproj(1, tok0, n, tkeys, lambda c_, obk=obk: list(obk))]
            P.phase = "yflush%d" % l
            for g_ in prev_bg:
                drain_gen(g_)
            prev_bg = []

            P.phase = "B2%d" % l
            load_w_cols(WIN, l, C_GB + 512, 512, 1536, "WIN")
            for k in range(8):
                P.dma("pool", lambda e, k=k, l=l: e.dma_start(out=W2[:, k, :], in_=w_out[l, k * 128:(k + 1) * 128, :]), writes=[("W2", k)], key=("w2", k))
            if last:
                P.dma("sp", lambda e: e.dma_start(out=GV[:], in_=fg), writes=["GV"])
            MIX = OA
            for (tok0, n, tiles) in b_groups:
                tkeys = [t0 // 128 for (t0, _) in tiles]
                P.dma("sp", lambda e, tok0=tok0, n=n: e.dma_start(out=HNT[:, :, 0:n], in_=hnts[:, :, tok0:tok0 + n]),
                      reads=[("hnts", t) for t in tkeys], writes=[("HNT", i) for i in range(4)], key=("HNTld",))
                hkeys = [("HNT", i) for i in range(4)]
                for dc in range(8):
                    YA, kya = f2()
                    YB, kyb = f2()
                    P.dma("sp", lambda e, dc=dc, tok0=tok0, n=n, YA=YA: e.dma_start(out=YA[:, 0:n], in_=yas[:, dc, tok0:tok0 + n]),
                          reads=[("yas", dc, t) for t in tkeys], writes=[kya])
                    P.dma("sp", lambda e, dc=dc, tok0=tok0, n=n, YB=YB: e.dma_start(out=YB[:, 0:n], in_=ybs[:, dc, tok0:tok0 + n]),
                          reads=[("ybs", dc, t) for t in tkeys], writes=[kyb])
                    ba, bb_ = (0, 1) if dc % 2 == 0 else (2, 3)
                    pa = bank(ba)[:, 0:n]
                    pb = bank(bb_)[:, 0:n]
                    for k in range(8):
                        P.op("pe", lambda e, k=k, dc=dc, n=n, pa=pa: e.matmul(pa, lhsT=WIN[:, k, dc * 128:(dc + 1) * 128], rhs=HNT[:, k, 0:n],
                                                                           start=(k == 0), stop=(k == 7)),
                             reads=hkeys + wk(dc * 128, 128, k), writes=[("ps", ba)])
                    for k in range(8):
                        P.op("pe", lambda e, k=k, dc=dc, n=n, pb=pb: e.matmul(pb, lhsT=WIN[:, k, 1024 + dc * 128:1024 + (dc + 1) * 128], rhs=HNT[:, k, 0:n],
                                                                           start=(k == 0), stop=(k == 7)),
                             reads=hkeys + wk(1024 + dc * 128, 128, k), writes=[("ps", bb_)])
                    TA, kta = f2()
                    TB, ktb = f2()
                    P.op("act", lambda e, n=n, pa=pa, TA=TA: e.activation(out=TA[:, 0:n], in_=pa, func=AF.Tanh, scale=0.5), reads=[("ps", ba)], writes=[kta])
                    P.op("act", lambda e, n=n, pb=pb, TB=TB: e.activation(out=TB[:, 0:n], in_=pb, func=AF.Tanh, scale=0.5), reads=[("ps", bb_)], writes=[ktb])
                    P.op("dve", lambda e, n=n, TA=TA, YA=YA: e.scalar_tensor_tensor(out=TA[:, 0:n], in0=TA[:, 0:n], scalar=1.0, in1=YA[:, 0:n],
                                                                                  op0=ALU.add, op1=ALU.mult), reads=[kta, kya], writes=[kta])
                    P.op("dve", lambda e, n=n, TB=TB, YB=YB: e.scalar_tensor_tensor(out=TB[:, 0:n], in0=TB[:, 0:n], scalar=1.0, in1=YB[:, 0:n],
                                                                                  op0=ALU.add, op1=ALU.mult), reads=[ktb, kyb], writes=[ktb])
                    P.op("pool", lambda e, dc=dc, n=n, TA=TA, TB=TB: e.tensor_tensor(out=MIX[:, dc, 0:n], in0=TA[:, 0:n], in1=TB[:, 0:n], op=ALU.add),
                         reads=[kta, ktb], writes=[("OA", dc)])
                mkeys = [("OA", dc) for dc in range(8)]
                for i, (t0, rows) in enumerate(tiles):
                    PY = PSB[2] if i % 2 == 0 else PSB[3]
                    pb0 = 4 if i % 2 == 0 else 6
                    HT, kht = f4()
                    HTf = HT[:].rearrange("p a b -> p (a b)")
                    src, skeys = hsrc(l, t0, rows)
                    P.dma("sp", lambda e, rows=rows, src=src, HTf=HTf: e.dma_start(out=HTf[0:rows, :], in_=src), reads=skeys, writes=[kht])
                    for half in range(2):
                        for k in range(8):
                            P.op("pe", lambda e, half=half, k=k, i=i, rows=rows, PY=PY: e.matmul(
                                PY[0:rows, half, :], lhsT=MIX[:, k, i * 128:i * 128 + rows], rhs=W2[:, k, half * 512:(half + 1) * 512],
                                start=(k == 0), stop=(k == 7)), reads=mkeys + [("W2", k)], writes=[("ps", pb0 + half)])
                    HNW, khw = f4()
                    P.op("dve", lambda e, rows=rows, HNW=HNW, HT=HT, PY=PY: e.scalar_tensor_tensor(out=HNW[0:rows], in0=PY[0:rows], scalar=0.5, in1=HT[0:rows],
                                                                                                 op0=ALU.mult, op1=ALU.add),
                         reads=[("ps", pb0), ("ps", pb0 + 1), kht], writes=[khw])
                    HNWf = HNW[:].rearrange("p a b -> p (a b)")
                    if not last:
                        P.dma("pool", lambda e, rows=rows, t0=t0, HNWf=HNWf: e.dma_start(out=hbuf[t0:t0 + rows, :], in_=HNWf[0:rows, :]),
                              reads=[khw], writes=[("hbuf", t0 // 128)], key=khw)
                    else:
                        OT, kot = f4()
                        OTf = OT[:].rearrange("p a b -> p (a b)")
                        ssq, kss = stcol()
                        P.op("pool", lambda e, rows=rows, ssq=ssq: e.memset(ssq[0:rows, :], 0.0), writes=[kss])
                        P.op("act", lambda e, rows=rows, OTf=OTf, HNWf=HNWf, ssq=ssq: e.activation(out=OTf[0:rows, :], in_=HNWf[0:rows, :], func=AF.Square,
                                                                                                   accum_out=ssq[0:rows, :]), reads=[khw], writes=[kot, kss])
                        rt, krt = stcol()
                        P.op("act", lambda e, rows=rows, rt=rt, ssq=ssq: e.activation(out=rt[0:rows, :], in_=ssq[0:rows, :], func=AF.Sqrt, bias=EPS, scale=1.0 / DM),
                             reads=[kss], writes=[krt])
                        rs, krs = stcol()
                        P.op("dve", lambda e, rows=rows, rs=rs, rt=rt: e.reciprocal(out=rs[0:rows, :], in_=rt[0:rows, :]), reads=[krt], writes=[krs])
                        P.op("dve", lambda e, rows=rows, OTf=OTf, HNWf=HNWf, rs=rs: e.scalar_tensor_tensor(
                            out=OTf[0:rows, :], in0=HNWf[0:rows, :], scalar=rs[0:rows, :], in1=GV[0:rows, :], op0=ALU.mult, op1=ALU.mult),
                            reads=[khw, krs, "GV"], writes=[kot])
                        P.dma("pool", lambda e, rows=rows, t0=t0, OTf=OTf: e.dma_start(out=out[t0:t0 + rows, :], in_=OTf[0:rows, :]),
                              reads=[kot], writes=[("out", t0 // 128)], key=kot)
        P.emit(nc)
    return nc, P


def _host_consts(na_rpb):
    W = 64
    c = np.arange(W)
    cs = np.clip(c - 8, 0, W - 16)
    ck = c[:, None]
    cq = c[None, :]
    valid = (ck >= cs[None, :]) & (ck < cs[None, :] + 16)
    bidx = np.clip(ck - cq + 15, 0, 30)
    mt = np.full((DEPTH, 16, 64, 8, 64), NEG, np.float32)
    for l in range(DEPTH):
        for i in range(15):
            g = na_rpb[l][:, i, :][:, bidx]
            g = np.where(valid[None], g, np.float32(NEG))
            mt[l, i] = g.transpose(1, 0, 2)
    mt = mt.reshape(DEPTH, 16, 64, 4, 2, 64).transpose(0, 1, 2, 4, 3, 5)
    return np.ascontiguousarray(mt)


def _rope_tables():
    t = np.arange(SEQ)
    row = np.concatenate([t // 64, np.zeros(NMETA, np.int64)]).astype(np.float32)
    col = np.concatenate([t % 64, np.zeros(NMETA, np.int64)]).astype(np.float32)
    inv = (np.float32(10000.0) ** (-np.arange(0, 32, 2, dtype=np.float32) / np.float32(32))).astype(np.float32)
    ar = row[:, None] * inv[None]
    ac = col[:, None] * inv[None]
    ang = np.concatenate([ar, ar, ac, ac], axis=-1).astype(np.float32)
    cos = np.cos(ang).astype(np.float32).T
    sin = np.sin(ang).astype(np.float32).T
    return np.ascontiguousarray(np.tile(cos, (2, 1))), np.ascontiguousarray(np.tile(sin, (2, 1)))


def _rot_mats():
    R = np.zeros((64, 64), np.float32)
    for d in range(16):
        R[d, d + 16] = -1.0
        R[d + 16, d] = 1.0
        R[32 + d, 48 + d] = -1.0
        R[48 + d, 32 + d] = 1.0
    lhsT = np.zeros((128, 128), np.float32)
    lhsT[:64, :64] = R.T
    lhsT[64:, 64:] = R.T
    bo = np.zeros((128, 128), np.float32)
    bo[:64, :64] = 1.0 / 64
    bo[64:, 64:] = 1.0 / 64
    return lhsT, bo


def _vpat():
    p = np.zeros((128, 194), np.float32)
    p[:, 64] = 1.0
    p[:, 66] = 1.0
    return p


_CACHE = {}


def kernel(x, meta_tokens, norm_g, w_in, q_norm_g, k_norm_g, na_rpb, w_o_attn, w_o_na, w_out, final_norm_g):
    x = np.asarray(x, np.float32)
    f = lambda a: np.ascontiguousarray(np.asarray(a, np.float32))
    if "nc" not in _CACHE:
        _CACHE["nc"] = build_nc()[0]
    nc = _CACHE["nc"]
    cosT, sinT = _rope_tables()
    rmat, bones = _rot_mats()
    shared = {
        "meta": f(meta_tokens),
        "ng": np.ascontiguousarray(np.broadcast_to(f(norm_g)[:, None, :], (DEPTH, 128, DM))),
        "w_in": f(w_in),
        "qg": np.ascontiguousarray(np.tile(f(q_norm_g), (1, 2))[:, :, None]),
        "kg": np.ascontiguousarray(np.tile(f(k_norm_g), (1, 2))[:, :, None]),
        "mtab": _host_consts(f(na_rpb)),
        "w_oa": f(w_o_attn), "w_ob": f(w_o_na), "w_out": f(w_out),
        "fg": np.ascontiguousarray(np.broadcast_to(f(final_norm_g)[None, :], (128, DM))),
        "cosT": cosT, "sinT": sinT, "rmat": rmat, "bones": bones, "eye": np.eye(128, dtype=np.float32),
        "vpat": _vpat(),
    }
    in_maps = []
    for b in range(NCORES):
        m = dict(shared)
        m["x"] = np.ascontiguousarray(x[b])
        in_maps.append(m)
    res = run_bass_kernel_spmd(nc, in_maps, core_ids=list(range(NCORES)))
    return np.stack([np.asarray(res.results[b]["out"], np.float32) for b in range(NCORES)], axis=0)
```
